# Optimizing a Trainium2 kernel written in Bass

```python
import math
import jax, jax.numpy as jnp
from jax import lax
import numpy as np

D_MODEL = 1024
BATCH = 4
SEQ = 8192
DEPTH = 1

HEAD_DIM = 64
RET_HEADS = 8
RET_WIDTH = RET_HEADS * HEAD_DIM
DIFF_HEADS = 4
DIFF_VDIM = 2 * HEAD_DIM
DIFF_WIDTH = DIFF_HEADS * DIFF_VDIM
MIX_WIDTH = RET_WIDTH + DIFF_WIDTH
DIFF_QK_WIDTH = 2 * DIFF_HEADS * HEAD_DIM
IN_WIDTH = 4 * RET_WIDTH + 2 * DIFF_QK_WIDTH + DIFF_WIDTH
D_FF = 2816
CHUNK = 128
Q_BLOCK = 128
ROPE_THETA = 10000.0
RET_THETA = 10000.0
LN_EPS = 1e-5
NORM_EPS = 1e-6
DEEPNORM_ALPHA = (2.0 * DEPTH) ** 0.25
DEEPNORM_BETA = (8.0 * DEPTH) ** -0.25

kernel_name = 'hybrid_retention_diffattn_macaron_deepnorm'

F32 = jnp.float32


def layer_norm(x, w, b):
    xf = x.astype(F32)
    mu = xf.mean(-1, keepdims=True)
    var = jnp.square(xf - mu).mean(-1, keepdims=True)
    y = (xf - mu) * lax.rsqrt(var + LN_EPS) * w.astype(F32) + b.astype(F32)
    return y.astype(x.dtype)


def swiglu(x, w_gate, w_up, w_down):
    return (jax.nn.silu(x @ w_gate) * (x @ w_up)) @ w_down


def rotary(x, positions):
    d = x.shape[-1]
    inv = 1.0 / (ROPE_THETA ** (jnp.arange(0, d, 2, dtype=F32) / d))
    ang = positions.astype(F32)[..., None] * inv
    c = jnp.cos(ang)[:, :, None, :]
    s = jnp.sin(ang)[:, :, None, :]
    x1, x2 = jnp.split(x.astype(F32), 2, axis=-1)
    return jnp.concatenate([x1 * c - x2 * s, x1 * s + x2 * c], axis=-1)


def retention_rotate(x, positions):
    d = x.shape[-1]
    inv = 1.0 / (RET_THETA ** jnp.linspace(0.0, 1.0, d // 2, dtype=F32))
    ang = positions.astype(F32)[..., None] * inv
    c = jnp.cos(ang)[:, :, None, :]
    s = jnp.sin(ang)[:, :, None, :]
    xf = x.astype(F32)
    xe, xo = xf[..., 0::2], xf[..., 1::2]
    return jnp.stack([xe * c - xo * s, xo * c + xe * s], axis=-1).reshape(xf.shape)


def retention_chunkwise(q, k, v):
    B, S, H, d = q.shape
    n_chunks = S // CHUNK
    log_g = jnp.log(1.0 - 2.0 ** (-5.0 - jnp.arange(H, dtype=F32)))
    idx = jnp.arange(CHUNK, dtype=F32)
    rel = idx[:, None] - idx[None, :]
    intra_decay = jnp.where(rel >= 0, jnp.exp(log_g[:, None, None] * jnp.maximum(rel, 0.0)), 0.0)
    q_decay = jnp.exp(log_g[:, None] * (idx + 1.0))
    k_decay = jnp.exp(log_g[:, None] * (CHUNK - 1.0 - idx))
    chunk_decay = jnp.exp(log_g * CHUNK)

    def to_chunks(t):
        return t.reshape(B, n_chunks, CHUNK, H, d).transpose(1, 0, 3, 2, 4)

    def step(state, inp):
        qi, ki, vi = inp
        scores = jnp.einsum('bhid,bhjd->bhij', qi, ki) * intra_decay
        intra = jnp.einsum('bhij,bhje->bhie', scores, vi)
        cross = jnp.einsum('bhid,bhde->bhie', qi, state) * q_decay[None, :, :, None]
        new_state = state * chunk_decay[None, :, None, None] + jnp.einsum('bhjd,hj,bhje->bhde', ki, k_decay, vi)
        return new_state, intra + cross

    state0 = jnp.zeros((B, H, d, d), F32)
    _, out = lax.scan(step, state0, (to_chunks(q), to_chunks(k), to_chunks(v)))
    return out.transpose(1, 0, 3, 2, 4).reshape(B, S, H, d)


def diff_attention(q, k, v, lam):
    B, S, H2, d = q.shape
    H = H2 // 2
    n_blk = S // Q_BLOCK
    scale = d ** -0.5
    key_pos = jnp.arange(S)
    qb = q.reshape(B, n_blk, Q_BLOCK, H2, d).transpose(1, 0, 3, 2, 4)

    def block(args):
        qi, start = args
        s = jnp.einsum('bhqd,bkhd->bhqk', qi, k) * scale
        qpos = start + jnp.arange(Q_BLOCK)
        s = jnp.where(key_pos[None, :] <= qpos[:, None], s, -jnp.inf)
        p = jax.nn.softmax(s, axis=-1).reshape(B, H, 2, Q_BLOCK, S)
        a = p[:, :, 0] - lam * p[:, :, 1]
        return jnp.einsum('bhqk,bkhe->bqhe', a, v)

    out = lax.map(block, (qb, jnp.arange(n_blk) * Q_BLOCK))
    return out.transpose(1, 0, 2, 3, 4).reshape(B, S, H, 2 * d)


def head_layernorm(y, w):
    B, S, H, e = y.shape
    mu = y.mean(-1, keepdims=True)
    var = jnp.square(y - mu).mean(-1, keepdims=True)
    return ((y - mu) * lax.rsqrt(var + NORM_EPS)).reshape(B, S, H * e) * w.astype(F32)


def head_rmsnorm(y, w):
    B, S, H, e = y.shape
    yn = y * lax.rsqrt(jnp.square(y).mean(-1, keepdims=True) + NORM_EPS)
    return yn.reshape(B, S, H * e) * w.astype(F32)


def token_mixer(x, positions, w_in, ret_norm_w, lq1, lk1, lq2, lk2, diff_norm_w, w_out, lambda_init):
    B, S, _ = x.shape
    h = x @ w_in
    splits = np.cumsum([RET_WIDTH] * 4 + [DIFF_QK_WIDTH] * 2)
    rq, rk, rv, rg, dq, dk, dv = jnp.split(h, splits, axis=-1)
    rq = retention_rotate(rq.reshape(B, S, RET_HEADS, HEAD_DIM), positions)
    rk = retention_rotate(rk.reshape(B, S, RET_HEADS, HEAD_DIM), positions) * (HEAD_DIM ** -0.5)
    rv = rv.reshape(B, S, RET_HEADS, HEAD_DIM).astype(F32)
    ret = retention_chunkwise(rq, rk, rv)
    ret = jax.nn.silu(rg.astype(F32)) * head_layernorm(ret, ret_norm_w)
    dq = rotary(dq.reshape(B, S, 2 * DIFF_HEADS, HEAD_DIM), positions)
    dk = rotary(dk.reshape(B, S, 2 * DIFF_HEADS, HEAD_DIM), positions)
    dv = dv.reshape(B, S, DIFF_HEADS, DIFF_VDIM).astype(F32)
    lam = (jnp.exp(jnp.sum(lq1.astype(F32) * lk1.astype(F32)))
           - jnp.exp(jnp.sum(lq2.astype(F32) * lk2.astype(F32))) + lambda_init)
    dif = diff_attention(dq, dk, dv, lam)
    dif = head_rmsnorm(dif, diff_norm_w) * (1.0 - lambda_init)
    merged = jnp.concatenate([ret, dif], axis=-1).astype(x.dtype)
    return merged @ w_out


def setup_inputs(seed: int = 0) -> dict:
    key = jax.random.key(seed)
    ks = jax.random.split(key, 24)
    L = DEPTH

    def nrm(k, shape, scale):
        return jax.random.normal(k, shape, F32) * scale

    def gain(k, n):
        return 1.0 + 0.02 * jax.random.normal(k, (L, n), F32)

    beta = DEEPNORM_BETA
    x = jax.random.normal(ks[0], (BATCH, SEQ, D_MODEL), F32)
    positions = jnp.broadcast_to(jnp.arange(SEQ, dtype=jnp.int32), (BATCH, SEQ))
    col_scale = np.ones((IN_WIDTH,), np.float32)
    col_scale[2 * RET_WIDTH:3 * RET_WIDTH] = beta
    col_scale[4 * RET_WIDTH + 2 * DIFF_QK_WIDTH:] = beta
    w_in = nrm(ks[1], (L, D_MODEL, IN_WIDTH), D_MODEL ** -0.5) * jnp.asarray(col_scale)
    return {
        'x': x,
        'positions': positions,
        'ffn1_w_gate': nrm(ks[2], (L, D_MODEL, D_FF), beta * D_MODEL ** -0.5),
        'ffn1_w_up': nrm(ks[3], (L, D_MODEL, D_FF), beta * D_MODEL ** -0.5),
        'ffn1_w_down': nrm(ks[4], (L, D_FF, D_MODEL), beta * D_FF ** -0.5),
        'ln1_w': gain(ks[5], D_MODEL),
        'ln1_b': nrm(ks[6], (L, D_MODEL), 0.02),
        'w_in': w_in,
        'ret_norm_w': gain(ks[7], RET_WIDTH),
        'diff_lambda_q1': nrm(ks[8], (L, HEAD_DIM), 0.1),
        'diff_lambda_k1': nrm(ks[9], (L, HEAD_DIM), 0.1),
        'diff_lambda_q2': nrm(ks[10], (L, HEAD_DIM), 0.1),
        'diff_lambda_k2': nrm(ks[11], (L, HEAD_DIM), 0.1),
        'diff_norm_w': gain(ks[12], DIFF_WIDTH),
        'w_out': nrm(ks[13], (L, MIX_WIDTH, D_MODEL), beta * MIX_WIDTH ** -0.5),
        'ln2_w': gain(ks[14], D_MODEL),
        'ln2_b': nrm(ks[15], (L, D_MODEL), 0.02),
        'ffn2_w_gate': nrm(ks[16], (L, D_MODEL, D_FF), beta * D_MODEL ** -0.5),
        'ffn2_w_up': nrm(ks[17], (L, D_MODEL, D_FF), beta * D_MODEL ** -0.5),
        'ffn2_w_down': nrm(ks[18], (L, D_FF, D_MODEL), beta * D_FF ** -0.5),
        'ln3_w': gain(ks[19], D_MODEL),
        'ln3_b': nrm(ks[20], (L, D_MODEL), 0.02),
    }


def reference(x, positions, ffn1_w_gate, ffn1_w_up, ffn1_w_down, ln1_w, ln1_b,
              w_in, ret_norm_w, diff_lambda_q1, diff_lambda_k1, diff_lambda_q2, diff_lambda_k2,
              diff_norm_w, w_out, ln2_w, ln2_b, ffn2_w_gate, ffn2_w_up, ffn2_w_down, ln3_w, ln3_b):
    alpha = DEEPNORM_ALPHA
    for l in range(DEPTH):
        lambda_init = 0.8 - 0.6 * math.exp(-0.3 * l)
        x = layer_norm(alpha * x + 0.5 * swiglu(x, ffn1_w_gate[l], ffn1_w_up[l], ffn1_w_down[l]), ln1_w[l], ln1_b[l])
        mix = token_mixer(x, positions, w_in[l], ret_norm_w[l], diff_lambda_q1[l], diff_lambda_k1[l],
                          diff_lambda_q2[l], diff_lambda_k2[l], diff_norm_w[l], w_out[l], lambda_init)
        x = layer_norm(alpha * x + mix, ln2_w[l], ln2_b[l])
        x = layer_norm(alpha * x + 0.5 * swiglu(x, ffn2_w_gate[l], ffn2_w_up[l], ffn2_w_down[l]), ln3_w[l], ln3_b[l])
    return x
```

```python
import math
import contextlib
import numpy as np
import concourse.bass as bass
import concourse.mybir as mybir
from concourse.bass_utils import run_bass_kernel_spmd

F32 = mybir.dt.float32
BF16 = mybir.dt.bfloat16
I32 = mybir.dt.int32
AF = mybir.ActivationFunctionType
ALU = mybir.AluOpType
AX = mybir.AxisListType

D = 1024
FF = 2816
SEQ = 8192
BATCH = 4
NBLK = SEQ // 128
NT = 16
INW = 3584
ALPHA = (2.0) ** 0.25
LAMBDA_INIT = 0.8 - 0.6 * math.exp(0.0)
LN_EPS = 1e-5
NORM_EPS = 1e-6
NEG_BIG = -30000.0
NFC = FF // 128


class Tracker:
    def __init__(self, nc, es):
        self.nc = nc
        self.es = es
        self.eng = {"pe": nc.tensor, "act": nc.scalar, "dve": nc.vector, "pool": nc.gpsimd, "sp": nc.sync}
        self.semh = {}
        self.cnt = {}
        for k in ["pe", "act", "dve", "pool"]:
            self.semh[k] = es.enter_context(nc.semaphore("s_" + k))
            self.cnt[k] = 0
        self.last_w = {}
        self.readers = {}
        self.waited = {e: {} for e in self.eng}
        self.n_inst = 0

    def dsem(self, name):
        key = "dma:" + name
        if key not in self.semh:
            self.semh[key] = self.es.enter_context(self.nc.semaphore("d_" + name))
            self.cnt[key] = 0
        return key

    def _wait(self, e, dep):
        key, val = dep
        if e == "pe" and key == "pe":
            return
        if self.waited[e].get(key, 0) >= val:
            return
        if key == "pe":
            assert val <= self.cnt["pe"], "dependency on unsignaled PE op"
        self.eng[e].wait_ge(self.semh[key], val)
        self.waited[e][key] = val

    def _deps(self, e, reads, writes):
        deps = []
        for r in reads:
            if r in self.last_w:
                deps.append(self.last_w[r])
        for w in writes:
            if w in self.last_w:
                deps.append(self.last_w[w])
            deps.extend(self.readers.get(w, ()))
        for d in deps:
            self._wait(e, d)

    def _record(self, me, reads, writes):
        for r in reads:
            self.readers.setdefault(r, []).append(me)
        for w in writes:
            self.last_w[w] = me
            self.readers[w] = []

    def op(self, e, fn, reads=(), writes=(), signal=True):
        self._deps(e, reads, writes)
        inst = fn(self.eng[e])
        self.n_inst += 1
        if e == "pe" and not signal:
            me = ("pe", self.cnt["pe"] + 1)
        else:
            self.cnt[e] += 1
            inst.then_inc(self.semh[e], 1)
            me = (e, self.cnt[e])
        self._record(me, reads, writes)

    def dma(self, out, in_, reads=(), writes=(), sem=None, q="sp"):
        key = self.dsem(sem)
        self._deps(q, reads, writes)
        inst = self.eng[q].dma_start(out=out, in_=in_)
        self.cnt[key] += 16
        inst.then_inc(self.semh[key], 16)
        self.n_inst += 1
        me = (key, self.cnt[key])
        self._record(me, reads, writes)

    def barrier_all(self, res):
        dep = self.last_w[res]
        for e in self.eng:
            self._wait(e, dep)


def build_program(nt=NT, stop=None, plan_in=None):
    nc = bass.Bass("TRN2", target_bir_lowering=False)

    def din(name, shape, dt=F32):
        return nc.dram_tensor(name, list(shape), dt, kind="ExternalInput").ap()

    def dint(name, shape, dt=BF16):
        return nc.dram_tensor(name, list(shape), dt, kind="Internal").ap()

    x_d = din("x", [SEQ, D])
    pos_d = din("pos", [128, NBLK], I32)
    wsrc = {
        "wg1": din("wg1", [D, FF]), "wu1": din("wu1", [D, FF]), "wd1": din("wd1", [FF, D]),
        "win": din("win", [D, INW]), "wout": din("wout", [D, D]),
        "wg2": din("wg2", [D, FF]), "wu2": din("wu2", [D, FF]), "wd2": din("wd2", [FF, D]),
    }
    lnv_d = din("lnv", [128, 6, D])
    nw_d = din("nw", [128, 2, 512])
    lam_d = din("lam", [128, 4, 64])
    DoT_d = din("DoT", [128, 8, 128])
    DxT_d = din("DxT", [128, 8, 128])
    qdec_d = din("qdec", [128, 4, 128])
    G_d = din("Gt", [128, 4, 64])
    kdec_d = din("kdec", [128, 2, 8])
    tri_d = din("tri", [128, 128])
    ident_d = din("ident", [128, 128])
    inv_d = din("inv", [128, 64])
    sbias_d = din("sbias", [128, 1])
    out_d = nc.dram_tensor("out", [SEQ // 2, D], F32, kind="ExternalOutput").ap()

    wb = {k: dint(k + "b", v.shape) for k, v in wsrc.items()}
    KTd = dint("KTd", [4, 128, SEQ])
    Vd = dint("Vd", [4, 128, NBLK, 130])

    es = contextlib.ExitStack()
    with es:
        def sb(name, shape, dt=F32):
            return es.enter_context(nc.sbuf_tensor("sb_" + name, list(shape), dt))

        tr = Tracker(nc, es)

        xf = sb("xf", [128, 6, D])
        xbf = [sb(f"xbf{i}", [128, D], BF16) for i in range(2)]
        xTA = sb("xTA", [128, 8, 512], BF16)
        xTB = sb("xTB", [128, 8, 512], BF16)
        hT = sb("hT", [128, NFC, 512], BF16)
        sg = [sb(f"sg{i}", [128, 512]) for i in range(2)]
        wbuf = [sb(f"wbuf{i}", [128, 4096], BF16) for i in range(4)]
        lnv = sb("lnv", [128, 6, D])
        nw = sb("nw", [128, 2, 512])
        DoT = sb("DoT", [128, 8, 128])
        DxT = sb("DxT", [128, 8, 128])
        qdec = sb("qdec", [128, 4, 128])
        Gt = sb("Gt", [128, 4, 64])
        kdec = sb("kdec", [128, 2, 8])
        tri = sb("tri", [128, 128], BF16)
        ident = sb("ident", [128, 128], BF16)
        inv = sb("inv", [128, 64])
        sbias = sb("sbias", [128, 1])
        posi = sb("posi", [128, NBLK], I32)
        posf = sb("posf", [128, NBLK])
        neghalf = sb("neghalf", [128, 8])
        neglam = sb("neglam", [128, 1])
        lams = sb("lams", [128, 4])
        sn = sb("sn", [128, 4, 64])
        cs = sb("cs", [128, 4, 64])
        bst = sb("bst", [128, 2, 6])
        mv = sb("mv", [128, 2])
        rstd = sb("rstd", [128, 1])
        pj = [sb(f"pj{i}", [128, 512]) for i in range(3)]
        _v = lambda tns, a, b: tns[:, a:b].rearrange("p (x y) -> p x y", x=4)
        ang, angk = _v(pj[0], 0, 256), _v(pj[0], 256, 512)
        angi_t = sb("angi", [128, 4, 64], I32)
        angi, angm = angi_t[:], _v(pj[1], 256, 512)
        rsn, rcs = _v(pj[2], 0, 256), _v(pj[2], 256, 512)
        ysq = pj[0][:].rearrange("p (h c) -> p h c", h=8)
        yr = pj[1][:].rearrange("p (h c) -> p h c", h=8)
        dfa, dfb = pj[2][:, 0:128], pj[2][:, 128:256]
        rt = [sb(f"rt{i}", [128, 256]) for i in range(4)]
        rot = [sb(f"rot{i}", [128, 512], BF16) for i in range(2)]
        rvb = sb("rvb", [128, 4, 512], BF16)
        Vaug = sb("Vaug", [128, 4, 4, 130], BF16)
        sgr = sb("sgr", [128, 2, 512])
        Kdec = sb("Kdec", [128, 4, 512], BF16)
        rkT = sb("rkT", [128, 4, 512], BF16)
        dkT = sb("dkT", [128, 4, 512], BF16)
        rqT = sb("rqT", [128, 4, 256], BF16)
        rqTd = sb("rqTd", [128, 4, 256], BF16)
        dqT = sb("dqT", [128, 4, 256], BF16)
        AoT = sb("AoT", [128, 8, 128], BF16)
        AxT = sb("AxT", [128, 8, 128], BF16)
        Sst = sb("Sst", [128, 4, 64])
        Sbf = sb("Sbf", [128, 4, 64], BF16)
        hst = sb("hst", [128, 4, 8])
        PT = [sb(f"PT{i}", [128, 1024], BF16) for i in range(2)]
        merged = sb("merged", [128, 2, D], BF16)

        banks = [es.enter_context(nc.psum_tensor(f"bank{i}", [128, 512], F32)) for i in range(8)]
        es_init = contextlib.ExitStack()
        sbi = lambda name, shape, dt=F32: es_init.enter_context(nc.sbuf_tensor("sb_" + name, list(shape), dt))
        lam = sbi("lam", [128, 4, 64])
        lamt = sbi("lamt", [128, 4, 64])
        trif = sbi("trif", [128, 128])
        identf = sbi("identf", [128, 128])

        def bk(i):
            return banks[i][:]

        def bkb(i):
            return banks[i][:].bitcast(BF16)

        def cast_pieces(name):
            src = wsrc[name]
            dst = wb[name]
            R, C = src.shape
            k = 1
            while C // k > 2048:
                k *= 2
            sv = src.rearrange("r (k c) -> (r k) c", k=k) if k > 1 else src
            dv = dst.rearrange("r (k c) -> (r k) c", k=k) if k > 1 else dst
            rows = R * k
            step = 512
            chunks = [(r0, min(rows, r0 + step)) for r0 in range(0, rows, step)]
            out = []
            for ci, (r0, r1) in enumerate(chunks):
                def piece(dep=None, r0=r0, r1=r1, lastp=(ci == len(chunks) - 1)):
                    tr.dma(out=dv[r0:r1, :], in_=sv[r0:r1, :], reads=[dep] if dep else [], writes=[], sem="c_" + name, q="pool")
                    if lastp:
                        tr.last_w["w_" + name] = ("dma:c_" + name, tr.cnt["dma:c_" + name])
                out.append(piece)
            return out

        def cast_weight(name):
            for p_ in cast_pieces(name):
                p_()

        pending_casts = []

        def drain_casts(n, dep=None):
            for _ in range(n):
                if pending_casts:
                    pending_casts.pop(0)(dep)

        consts = [(lnv, lnv_d), (nw, nw_d), (lam, lam_d), (DoT, DoT_d), (DxT, DxT_d), (qdec, qdec_d),
                  (Gt, G_d), (kdec, kdec_d), (trif, tri_d), (identf, ident_d), (inv, inv_d),
                  (sbias, sbias_d), (posi, pos_d)]
        for t_, d_ in consts:
            tr.dma(out=t_[:], in_=d_, writes=["consts"], sem="consts")
        for name in ["wg1", "wu1", "wd1"]:
            cast_weight(name)
        for name in ["win", "wout", "wg2", "wu2", "wd2"]:
            pending_casts.extend(cast_pieces(name))
        tr.barrier_all("consts")
        del tr.last_w["consts"]

        tr.op("dve", lambda e: e.tensor_copy(out=tri[:], in_=trif[:]), writes=["tri"])
        tr.op("dve", lambda e: e.tensor_copy(out=ident[:], in_=identf[:]), writes=["ident"])
        tr.op("dve", lambda e: e.tensor_copy(out=posf[:], in_=posi[:]), writes=["posf"])
        tr.op("dve", lambda e: e.memset(neghalf[:], -0.5), writes=["neghalf"])
        tr.op("dve", lambda e: e.memset(Vaug[:].rearrange("p a b c -> p (a b) c")[:, :, 128:130], 1.0), writes=["Vaug"])
        tr.op("dve", lambda e: e.memset(Sst[:], 0.0), writes=["Sst"])
        tr.op("dve", lambda e: e.memset(Sbf[:], 0.0), writes=["Sbf"])
        tr.op("dve", lambda e: e.tensor_tensor(out=lamt[:, 0, :], in0=lam[:, 0, :], in1=lam[:, 1, :], op=ALU.mult), writes=["lamt"])
        tr.op("dve", lambda e: e.tensor_tensor(out=lamt[:, 1, :], in0=lam[:, 2, :], in1=lam[:, 3, :], op=ALU.mult), writes=["lamt"])
        tr.op("dve", lambda e: e.tensor_reduce(out=lams[:, 0:2], in_=lamt[:, 0:2, :], axis=AX.X, op=ALU.add), reads=["lamt"], writes=["lams"])
        tr.op("act", lambda e: e.activation(out=lams[:, 2:4], in_=lams[:, 0:2], func=AF.Exp), reads=[], writes=["lams"])
        tr.op("dve", lambda e: e.tensor_tensor(out=neglam[:], in0=lams[:, 3:4], in1=lams[:, 2:3], op=ALU.subtract), reads=["lams"], writes=["neglam"])
        tr.op("dve", lambda e: e.tensor_scalar(out=neglam[:], in0=neglam[:], scalar1=-float(LAMBDA_INIT), scalar2=None, op0=ALU.add), writes=["neglam"])
        for r_ in ["tri", "ident", "posf", "neghalf", "neglam", "Vaug"]:
            tr.barrier_all(r_)
            del tr.last_w[r_]
            tr.readers.pop(r_, None)

        for r_ in ["lams", "lamt"]:
            tr.barrier_all(r_)
        es_init.close()
        KTp = [sb(f"KTp{i}", [128, 512], BF16) for i in range(2)]
        Vp = [sb(f"Vp{i}", [128, 4, 130], BF16) for i in range(2)]
        ob = sb("ob", [128, 2, 130])
        dst_ = sb("dst", [128, 8])

        state = {"gu_mode": "D", "gu_half": 0, "lastS": (0, 1), "rot": 0, "tb": 0, "slab": 0, "gu": 0, "pjs": 0, "wbank": 0, "sbank": 0, "pt": 0, "kv": 0, "xbf": 0}

        CG_ORDER = [1, 2, 5, 6, 0, 3, 4]
        recording = plan_in is None
        plan = [] if recording else list(plan_in)
        ring_state = {"issued": 0, "consumed": 0, "done": 0}

        def gview(i):
            return wbuf[i][:].rearrange("p (dc c) -> p dc c", dc=8)

        def dview(i):
            return wbuf[i][:].rearrange("p (fc c) -> p fc c", fc=4)

        def _issue(k):
            kind, wn, c0, cw = plan[k]
            i = k % 4
            if kind == "g":
                tr.dma(out=gview(i)[:, :, :cw], in_=wb[wn][:, c0:c0 + cw].rearrange("(dc p) c -> p dc c", p=128),
                       reads=["w_" + wn], writes=[f"wb{i}"], sem=f"wb{i}")
            else:
                tr.dma(out=dview(i)[:, :cw // 128, :], in_=wb[wn][c0:c0 + cw, :].rearrange("(fc p) c -> p fc c", p=128),
                       reads=["w_" + wn], writes=[f"wb{i}"], sem=f"wb{i}")

        def need(kind, wn, c0, cw=512):
            k = ring_state["consumed"]
            if recording:
                plan.append((kind, wn, c0, cw))
            assert plan[k] == (kind, wn, c0, cw), (plan[k], kind, wn, c0, cw)
            ring_state["consumed"] += 1
            _pump()
            assert ring_state["issued"] > k, "weight-slab ring exhausted: more than 4 slabs live"
            live_k[k % 4] = k
            return k % 4

        def _pump():
            while ring_state["issued"] < min(len(plan), ring_state["done"] + 4):
                _issue(ring_state["issued"])
                ring_state["issued"] += 1

        live_k = {}
        done_set = set()

        def release(*bufs):
            for b_ in bufs:
                done_set.add(live_k[b_])
            while ring_state["done"] in done_set:
                done_set.discard(ring_state["done"])
                ring_state["done"] += 1
            _pump()

        def transpose_blocks(src_ap_fn, n, dst_ap, src_res, dst_res, evac="act"):
            b = state["tb"] % 2
            state["tb"] += 1
            pv = bkb(b)
            for j in range(n):
                tr.op("pe", lambda e, j=j: e.transpose(out=pv[:, j * 128:(j + 1) * 128], in_=src_ap_fn(j), identity=ident[:]),
                      reads=src_res, writes=[f"ps{b}"], signal=(j == n - 1))
            src = pv[:, 0:n * 128].rearrange("p (a b) -> p a b", a=n)
            if evac == "act":
                tr.op("act", lambda e: e.copy(out=dst_ap, in_=src), writes=[f"ps{b}"] + dst_res)
            else:
                tr.op("dve", lambda e: e.tensor_copy(out=dst_ap, in_=src), writes=[f"ps{b}"] + dst_res)

        def stream_to_T(slot, dstT, dpre, col0, scale_alpha, light_act=False):
            s = state["xbf"] % 2
            state["xbf"] += 1
            if light_act:
                tr.op("dve", lambda e: e.tensor_copy(out=xbf[s][:], in_=xf[:, slot, :]), reads=[f"xf{slot}"], writes=[f"xbf{s}"])
            else:
                tr.op("act", lambda e: e.copy(out=xbf[s][:], in_=xf[:, slot, :]), reads=[f"xf{slot}"], writes=[f"xbf{s}"])
            ev = "dve"
            transpose_blocks(lambda j: xbf[s][:, j * 128:(j + 1) * 128], 8, dstT[:, :, col0:col0 + 128],
                             [f"xbf{s}"], [f"{dpre}{col0 // 128}"], evac=ev)
            if scale_alpha:
                if light_act:
                    tr.op("dve", lambda e: e.tensor_scalar(out=xf[:, slot, :], in0=xf[:, slot, :], scalar1=float(ALPHA), scalar2=None,
                                                            op0=ALU.mult), writes=[f"xf{slot}"])
                else:
                    tr.op("act", lambda e: e.mul(out=xf[:, slot, :], in_=xf[:, slot, :], mul=float(ALPHA)), writes=[f"xf{slot}"])

        def layer_norm(blk, iw, ib):
            src = xf[:, blk, :]
            res = f"xf{blk}"
            tr.op("dve", lambda e: e.bn_stats(out=bst[:, 0, :], in_=xf[:, blk, 0:512]), reads=[res], writes=["bst"])
            tr.op("dve", lambda e: e.bn_stats(out=bst[:, 1, :], in_=xf[:, blk, 512:1024]), reads=[res], writes=["bst"])
            tr.op("dve", lambda e: e.bn_aggr(out=mv[:], in_=bst[:].rearrange("p a b -> p (a b)")), reads=["bst"], writes=["mv"])
            tr.op("dve", lambda e: e.tensor_scalar(out=rstd[:], in0=mv[:, 1:2], scalar1=float(LN_EPS), scalar2=None, op0=ALU.add),
                  reads=["mv"], writes=["rstd"])
            tr.op("pool", lambda e: e.tensor_tensor(out=rstd[:], in0=rstd[:], in1=neghalf[:, 0:1], op=ALU.pow), writes=["rstd"])
            tr.op("dve", lambda e: e.scalar_tensor_tensor(out=src, in0=src, scalar=mv[:, 0:1], in1=lnv[:, iw, :],
                                                          op0=ALU.subtract, op1=ALU.mult), reads=["mv"], writes=[res])
            tr.op("dve", lambda e: e.scalar_tensor_tensor(out=src, in0=src, scalar=rstd[:, 0:1], in1=lnv[:, ib, :],
                                                          op0=ALU.mult, op1=ALU.add), reads=["rstd"], writes=[res])

        def ffn_gu(wn_g, wn_u, ntok, srcT, src_res, inter=False):
            nslab = 6
            for j in range(nslab):
                cw = 512 if j < 5 else 256
                ig = need("g", wn_g, j * 512, cw)
                iu = need("g", wn_u, j * 512, cw)
                wgv, wuv = gview(ig), gview(iu)
                for k in range(cw // 128):
                    fc = j * 4 + k
                    q = state["gu"] % 2
                    state["gu"] += 1
                    mode = state["gu_mode"] if inter else "N"
                    if mode == "D":
                        bg, bu = (6, 7) if q == 0 else (0, 1)
                    elif mode == "B":
                        bg, bu = 6, 7
                    elif mode == "C":
                        bg, bu = 0, 1
                    else:
                        bg, bu = (2, 3) if q == 0 else (4, 5)
                    state["gu_half"] = 1
                    for dc in range(8):
                        tr.op("pe", lambda e, dc=dc: e.matmul(bk(bg)[:, :ntok], lhsT=wgv[:, dc, k * 128:(k + 1) * 128],
                                                              rhs=srcT[:, dc, :ntok], start=(dc == 0), stop=(dc == 7)),
                              reads=[f"wb{ig}"] + src_res, writes=[f"ps{bg}"], signal=(dc == 7))
                    if inter:
                        yield
                    for dc in range(8):
                        tr.op("pe", lambda e, dc=dc: e.matmul(bk(bu)[:, :ntok], lhsT=wuv[:, dc, k * 128:(k + 1) * 128],
                                                              rhs=srcT[:, dc, :ntok], start=(dc == 0), stop=(dc == 7)),
                              reads=[f"wb{iu}"] + src_res, writes=[f"ps{bu}"], signal=(dc == 7))
                    sq = sg[q][:, :ntok]
                    if mode == "D":
                        tr.op("act", lambda e: e.activation(out=sq, in_=bk(bg)[:, :ntok], func=AF.Tanh, scale=0.5),
                              reads=[f"ps{bg}"], writes=[f"sg{q}"])
                        tr.op("dve", lambda e: e.scalar_tensor_tensor(out=sq, in0=sq, scalar=1.0, in1=bk(bg)[:, :ntok],
                                                                      op0=ALU.add, op1=ALU.mult), writes=[f"ps{bg}", f"sg{q}"])
                        tr.op("dve", lambda e: e.scalar_tensor_tensor(out=hT[:, fc, :ntok], in0=sq, scalar=0.5, in1=bk(bu)[:, :ntok],
                                                                      op0=ALU.mult, op1=ALU.mult),
                              reads=[f"sg{q}"], writes=[f"ps{bu}", f"hT{fc}"])
                    else:
                        tr.op("act", lambda e: e.activation(out=sq, in_=bk(bg)[:, :ntok], func=AF.Silu),
                              writes=[f"ps{bg}", f"sg{q}"])
                        tr.op("dve", lambda e: e.tensor_tensor(out=hT[:, fc, :ntok], in0=sq, in1=bk(bu)[:, :ntok], op=ALU.mult),
                              reads=[f"sg{q}"], writes=[f"ps{bu}", f"hT{fc}"])
                        if fc % 2 == 1:
                            drain_casts(1, dep=f"hT{fc}")
                    state["gu_half"] = 0
                    if inter:
                        yield
                release(ig, iu)

        def ffn_down(wn_d, blks, post=None, between=None):
            nslab = 6
            if between is not None:
                between()
            npass = (len(blks) + 1) // 2
            for ps_ in range(npass):
                pblks = blks[ps_ * 2:ps_ * 2 + 2]
                bsets = [(0, 1), (6, 7)] if ps_ % 2 == 0 else [(2, 3), (4, 5)]
                for j in range(nslab):
                    cw = 512 if j < 5 else 256
                    s = need("d", wn_d, j * 512, cw)
                    wdv = dview(s)
                    for bi, (blk, tcol) in enumerate(pblks):
                        for k in range(cw // 128):
                            fc = j * 4 + k
                            for half in range(2):
                                b = bsets[bi][half]
                                tr.op("pe", lambda e, b=b, half=half, fc=fc, k=k, tcol=tcol, wdv=wdv: e.matmul(
                                    bk(b), lhsT=hT[:, fc, tcol:tcol + 128], rhs=wdv[:, k, half * 512:(half + 1) * 512],
                                    start=(fc == 0), stop=(fc == NFC - 1)),
                                    reads=[f"wb{s}", f"hT{fc}"], writes=[f"ps{b}"], signal=(fc == NFC - 1 or k == cw // 128 - 1))
                    release(s)
                    drain_casts(1)
                for bi, (blk, tcol) in enumerate(pblks):
                    for half in range(2):
                        b = bsets[bi][half]
                        dst = xf[:, blk, half * 512:(half + 1) * 512]
                        tr.op("dve", lambda e, b=b, dst=dst: e.scalar_tensor_tensor(out=dst, in0=bk(b), scalar=0.5, in1=dst,
                                                                                    op0=ALU.mult, op1=ALU.add),
                              writes=[f"ps{b}", f"xf{blk}"])
                    if post is not None and ps_ < npass - 1:
                        post(blk)

        def rot_tables(t, defer_sin=False):
            twopi = 2.0 * math.pi
            c1 = float(np.float32(6.28125))
            c2 = float(np.float32(twopi - c1))
            c3 = float(twopi - c1 - c2)
            V = "dve"
            PJ3 = ["pj0", "pj1", "pj2"]
            tr.op(V, lambda e: e.tensor_tensor(out=ang, in0=posf[:, 4 * t:4 * t + 4].unsqueeze(2).to_broadcast([128, 4, 64]),
                                               in1=inv[:].unsqueeze(1).to_broadcast([128, 4, 64]), op=ALU.mult), writes=PJ3)
            tr.op(V, lambda e: e.tensor_scalar(out=angi, in0=ang, scalar1=float(1.0 / twopi), scalar2=None, op0=ALU.mult),
                  writes=PJ3)
            tr.op(V, lambda e: e.tensor_copy(out=angk, in_=angi), writes=PJ3)
            tr.op(V, lambda e: e.scalar_tensor_tensor(out=rsn, in0=angk, scalar=-c1, in1=ang, op0=ALU.mult, op1=ALU.add),
                  writes=PJ3)
            tr.op(V, lambda e: e.scalar_tensor_tensor(out=rsn, in0=angk, scalar=-c2, in1=rsn, op0=ALU.mult, op1=ALU.add),
                  writes=PJ3)
            tr.op(V, lambda e: e.scalar_tensor_tensor(out=rsn, in0=angk, scalar=-c3, in1=rsn, op0=ALU.mult, op1=ALU.add),
                  writes=PJ3)

            def wrap(dst, dres, src, sres, shift):
                tr.op(V, lambda e: e.tensor_scalar(out=dst, in0=src, scalar1=float(shift), scalar2=None, op0=ALU.add),
                      writes=PJ3)
                tr.op(V, lambda e: e.tensor_scalar(out=angm, in0=dst, scalar1=float(-math.pi), scalar2=float(twopi),
                                                   op0=ALU.is_lt, op1=ALU.mult), writes=PJ3)
                tr.op(V, lambda e: e.tensor_tensor(out=dst, in0=dst, in1=angm, op=ALU.add), writes=PJ3)
                tr.op(V, lambda e: e.tensor_scalar(out=angm, in0=dst, scalar1=float(math.pi), scalar2=float(-twopi),
                                                   op0=ALU.is_gt, op1=ALU.mult), writes=PJ3)
                tr.op(V, lambda e: e.tensor_tensor(out=dst, in0=dst, in1=angm, op=ALU.add), writes=PJ3)
                tr.op(V, lambda e: e.tensor_scalar(out=dst, in0=dst, scalar1=float(math.pi), scalar2=float(-math.pi),
                                                   op0=ALU.min, op1=ALU.max), writes=PJ3)
            wrap(rsn, "rsn", rsn, "rsn", 0.0)
            wrap(rcs, "rcs", rsn, "rsn", math.pi / 2)
            if not defer_sin:
                rot_sin()

        def rot_sin():
            tr.op("act", lambda e: e.activation(out=sn[:], in_=rsn, func=AF.Sin), writes=["pj2", "sn"])
            tr.op("act", lambda e: e.activation(out=cs[:], in_=rcs, func=AF.Sin), writes=["pj2", "cs"])

        def rotate(src, sres, dst, dres, blk, kind):
            if kind == "ret":
                sv = src.rearrange("p (h i two) -> p h i two", h=8, i=32, two=2)
                dv = dst.rearrange("p (h i two) -> p h i two", h=8, i=32, two=2)
                a, b_ = sv[:, :, :, 0], sv[:, :, :, 1]
                oa, ob_ = dv[:, :, :, 0], dv[:, :, :, 1]
                c = cs[:, blk, 0:32].unsqueeze(1).to_broadcast([128, 8, 32])
                s_ = sn[:, blk, 0:32].unsqueeze(1).to_broadcast([128, 8, 32])
            else:
                sv = src.rearrange("p (h two i) -> p h two i", h=8, two=2, i=32)
                dv = dst.rearrange("p (h two i) -> p h two i", h=8, two=2, i=32)
                a, b_ = sv[:, :, 0, :], sv[:, :, 1, :]
                oa, ob_ = dv[:, :, 0, :], dv[:, :, 1, :]
                c = cs[:, blk, 32:64].unsqueeze(1).to_broadcast([128, 8, 32])
                s_ = sn[:, blk, 32:64].unsqueeze(1).to_broadcast([128, 8, 32])
            t = [r_[:].rearrange("p (h i) -> p h i", h=8) for r_ in rt]
            tr.op("dve", lambda e: e.tensor_tensor(out=t[0], in0=a, in1=c, op=ALU.mult), reads=[sres, "cs"], writes=["rt0"])
            tr.op("dve", lambda e: e.tensor_tensor(out=t[1], in0=b_, in1=s_, op=ALU.mult), reads=[sres, "sn"], writes=["rt1"])
            tr.op("dve", lambda e: e.tensor_tensor(out=t[2], in0=b_, in1=c, op=ALU.mult), reads=[sres, "cs"], writes=["rt2"])
            tr.op("dve", lambda e: e.tensor_tensor(out=t[3], in0=a, in1=s_, op=ALU.mult), reads=[sres, "sn"], writes=["rt3"])
            tr.op("dve", lambda e: e.tensor_tensor(out=oa, in0=t[0], in1=t[1], op=ALU.subtract), reads=["rt0", "rt1"], writes=[dres])
            tr.op("dve", lambda e: e.tensor_tensor(out=ob_, in0=t[2], in1=t[3], op=ALU.add), reads=["rt2", "rt3"], writes=[dres])

        def win_slab(cg):
            return need("g", "win", cg * 512)

        def proj(s, tcol):
            b = 2 + state["wbank"] % 4
            state["wbank"] += 1
            for dc in range(8):
                tr.op("pe", lambda e, dc=dc: e.matmul(bk(b), lhsT=xTB[:, dc, tcol:tcol + 128], rhs=gview(s)[:, dc, :],
                                                      start=(dc == 0), stop=(dc == 7)),
                      reads=[f"wb{s}", f"xTB{tcol // 128}"], writes=[f"ps{b}"], signal=(dc == 7))
            return b

        def evac_f32(b):
            i = state["pjs"] % 3
            state["pjs"] += 1
            tr.op("act", lambda e: e.copy(out=pj[i][:], in_=bk(b)), writes=[f"ps{b}", f"pj{i}"])
            return i

        OWN_SLOTS = [(0, 1), (2, 3)]
        own_blks = [0, 2]

        def slots(t):
            o = OWN_SLOTS[t % 2]
            return [o[0], 4, o[1], 5]

        def load_x(t):
            sl4 = slots(t)
            for blk in range(4):
                L = 4 * t + blk
                tr.dma(out=xf[:, sl4[blk], :], in_=x_d[L * 128:(L + 1) * 128, :], writes=[f"xf{sl4[blk]}"], sem=f"xf{sl4[blk]}")

        def ffn(wn_g, wn_u, wn_d, ntok, blks, srcT, src_res, post=None, between=None):
            for _ in ffn_gu(wn_g, wn_u, ntok, srcT, src_res):
                pass
            ffn_down(wn_d, blks, post=post, between=between)

        def prep_A_blk(t, blk, light_act=False):
            sl4 = slots(t)
            stream_to_T(sl4[blk], xTA, "xTA", blk * 128, True, light_act=light_act)

        def prep_A(t, light_act=False):
            rot_tables(t)
            for blk in range(4):
                prep_A_blk(t, blk, light_act=light_act)

        def ffn_A_gu(t, inter=False):
            return ffn_gu("wg1", "wu1", 512, xTA, [f"xTA{i}" for i in range(4)], inter=inter)

        def ffn_A_down(t, between=None):
            sl4 = slots(t)
            ffn_down("wd1", [(sl4[b_], b_ * 128) for b_ in range(4)], post=lambda slot: layer_norm(slot, 0, 1), between=between)
            layer_norm(sl4[2], 0, 1)
            layer_norm(sl4[3], 0, 1)

        def x1_to_T(t):
            sl4 = slots(t)
            for blk in range(4):
                stream_to_T(sl4[blk], xTB, "xTB", blk * 128, blk in own_blks)

        def x2_to_T(t):
            sl4 = slots(t)
            for qi, blk in enumerate(own_blks):
                stream_to_T(sl4[blk], xTB, "xTB", qi * 128, True)

        def stage_BE(t, before_C=None, before_D=None, inter_gen=None):
            sl4 = slots(t)
            if t + 1 < nt:
                load_x(t + 1)
            items = []
            for blk in range(4):
                items.append(("rk", 1, blk, None))
            for blk in range(4):
                items.append(("rv", 2, blk, None))
            for blk in range(4):
                items.append(("dk", 5, blk, None))
            for blk in range(4):
                items.append(("dv", 6, blk, None))
            for qi, blk in enumerate(own_blks):
                items.append(("rq", 0, blk, qi))
            for qi, blk in enumerate(own_blks):
                items.append(("rg", 3, blk, qi))
            for qi, blk in enumerate(own_blks):
                items.append(("dq", 4, blk, qi))
            cg_order = [1, 2, 5, 6, 0, 3, 4]
            slab_of = {}

            prev_cg = [None]

            def issue_slab(cg):
                slab_of[cg] = win_slab(cg)
                prev_cg[0] = cg

            st_ = {}

            def P(it):
                kind, cg, blk, qi = it
                if cg not in slab_of:
                    if slab_of:
                        release(slab_of[prev_cg[0]])
                    issue_slab(cg)
                nxt = cg_order.index(cg) + 1
                if nxt < len(cg_order) and cg_order[nxt] not in slab_of and blk == (3 if qi is None else own_blks[-1]):
                    pass
                st_[it] = dict(bank=proj(slab_of[cg], blk * 128))

            def V(it):
                kind, cg, blk, qi = it
                b = st_[it]["bank"]
                if kind == "rv":
                    tr.op("act", lambda e: e.copy(out=rvb[:, blk, :], in_=bk(b)), writes=[f"ps{b}", f"rvb{blk}"])
                elif kind == "dv":
                    tr.op("act", lambda e: e.copy(out=Vaug[:, blk, :, 0:128], in_=bk(b).rearrange("p (h c) -> p h c", h=4)),
                          writes=[f"ps{b}", f"Vaug{blk}"])
                elif kind == "rg":
                    tr.op("act", lambda e: e.activation(out=sgr[:, qi, :], in_=bk(b), func=AF.Silu), writes=[f"ps{b}", f"sgr{qi}"])
                else:
                    i = evac_f32(b)
                    r = state["rot"] % 2
                    state["rot"] += 1
                    st_[it]["rot"] = r
                    rotate(pj[i][:], f"pj{i}", rot[r][:], f"rot{r}", blk, "ret" if kind in ("rk", "rq") else "rope")
                    if kind == "rk":
                        which = 0 if blk in own_blks else 1
                        tr.op("dve", lambda e: e.tensor_tensor(
                            out=Kdec[:, blk, :].rearrange("p (h d) -> p h d", h=8), in0=rot[r][:].rearrange("p (h d) -> p h d", h=8),
                            in1=kdec[:, which, :].unsqueeze(2).to_broadcast([128, 8, 64]), op=ALU.mult),
                            reads=[f"rot{r}"], writes=[f"Kdec{blk}"])

            def T(it):
                kind, cg, blk, qi = it
                if kind in ("rv", "dv", "rg"):
                    return
                r = st_[it]["rot"]
                src = lambda j: rot[r][:, j * 128:(j + 1) * 128]
                if kind == "rk":
                    transpose_blocks(src, 4, rkT[:, :, blk * 128:(blk + 1) * 128], [f"rot{r}"], [f"rkT{blk}"], evac="act")
                elif kind == "dk":
                    transpose_blocks(src, 4, dkT[:, :, blk * 128:(blk + 1) * 128], [f"rot{r}"], [f"dkT{blk}"], evac="act")
                elif kind == "rq":
                    transpose_blocks(src, 4, rqT[:, :, qi * 128:(qi + 1) * 128], [f"rot{r}"], [f"rqT{qi}"], evac="act")
                    tr.op("dve", lambda e: e.tensor_tensor(out=rqTd[:, :, qi * 128:(qi + 1) * 128], in0=rqT[:, :, qi * 128:(qi + 1) * 128],
                                                            in1=qdec[:], op=ALU.mult), reads=[f"rqT{qi}"], writes=[f"rqTd{qi}"])
                elif kind == "dq":
                    transpose_blocks(src, 4, dqT[:, :, qi * 128:(qi + 1) * 128], [f"rot{r}"], [f"dqT{qi}"], evac="act")

            def pump(n):
                if inter_gen is not None:
                    for _ in range(n):
                        next(inter_gen, None)

            state["gu_mode"] = "B"
            n_it = len(items)
            for idx in range(n_it + 2):
                if idx < n_it:
                    cg = items[idx][1]
                    P(items[idx])
                    if inter_gen is not None:
                        if 4 <= idx < 8:
                            prep_A_blk(t + 1, idx - 4)
                if 0 <= idx - 1 < n_it:
                    V(items[idx - 1])
                if 0 <= idx - 2 < n_it:
                    T(items[idx - 2])
            release(slab_of[prev_cg[0]])
            if before_C is not None:
                before_C()
            for h in range(4):
                tr.dma(out=KTd[h, :, t * 512:(t + 1) * 512], in_=dkT[:, h, :], reads=[f"dkT{i}" for i in range(4)],
                       writes=[f"kvd{t}"], sem=f"kvw{t % 2}")
                tr.dma(out=Vd[h, :, 4 * t:4 * t + 4, :], in_=Vaug[:, :, h, :], reads=[f"Vaug{i}" for i in range(4)],
                       writes=[f"kvd{t}"], sem=f"kvw{t % 2}")
            if state["gu_half"] == 1:
                pump(1)
            state["gu_mode"] = "C"
            if before_D is not None:
                rot_tables(t + 1, defer_sin=True)
            for qi in range(2):
                ob_, xb_ = 2 * qi, 2 * qi + 1
                for (kb, bA, bB, Dt, At, ares) in [(ob_, 2, 3, DoT, AoT, "AoT"), (xb_, 4, 5, DxT, AxT, "AxT")]:
                    for h in range(8):
                        i_, hh = h // 2, h % 2
                        b = bA if hh == 0 else bB
                        tr.op("pe", lambda e, b=b, h=h, i_=i_, hh=hh, kb=kb: e.matmul(
                            bk(b)[:, i_ * 128:(i_ + 1) * 128],
                            lhsT=rkT[hh * 64:(hh + 1) * 64, i_, kb * 128:(kb + 1) * 128],
                            rhs=rqT[hh * 64:(hh + 1) * 64, i_, qi * 128:(qi + 1) * 128], start=True, stop=True),
                            reads=[f"rkT{kb}", f"rqT{qi}"], writes=[f"ps{b}"], signal=(h >= 6))
                    for half, b in enumerate([bA, bB]):
                        tr.op("dve", lambda e, b=b, half=half, Dt=Dt, At=At: e.tensor_tensor(
                            out=At[:, half * 4:(half + 1) * 4, :], in0=bk(b).rearrange("p (h q) -> p h q", h=4),
                            in1=Dt[:, half * 4:(half + 1) * 4, :], op=ALU.mult), writes=[f"ps{b}", ares + str(half)])
                first = True
                for h in range(8):
                    i_, hh = h // 2, h % 2
                    half = hh
                    sl_ = hh * 4 + i_
                    oc = bk(6)[:, h * 64:(h + 1) * 64]
                    tr.op("pe", lambda e, oc=oc, h=h, first=first, sl_=sl_: e.matmul(oc, lhsT=AoT[:, sl_, :], rhs=rvb[:, ob_, h * 64:(h + 1) * 64],
                                                                            start=first, stop=False, skip_group_check=True),
                          reads=[f"AoT{half}", f"rvb{ob_}"], writes=["ps6"], signal=False)
                    first = False
                    tr.op("pe", lambda e, oc=oc, h=h, sl_=sl_: e.matmul(oc, lhsT=AxT[:, sl_, :], rhs=rvb[:, xb_, h * 64:(h + 1) * 64],
                                                               start=False, stop=False, skip_group_check=True),
                          reads=[f"AxT{half}", f"rvb{xb_}"], writes=["ps6"], signal=False)
                    tr.op("pe", lambda e, oc=oc, h=h, i_=i_, hh=hh: e.matmul(
                        oc, lhsT=rqTd[hh * 64:(hh + 1) * 64, i_, qi * 128:(qi + 1) * 128], rhs=Sbf[hh * 64:(hh + 1) * 64, i_, :],
                        start=False, stop=(h == 7), skip_group_check=True),
                        reads=[f"rqTd{qi}", "Sbf"], writes=["ps6"], signal=(h == 7))
                pump(3)
                for i_ in range(4):
                    for n_, kb in enumerate([ob_, xb_]):
                        tr.op("pe", lambda e, i_=i_, kb=kb, n_=n_: e.matmul(
                            bk(7)[:, i_ * 128:(i_ + 1) * 128], lhsT=Kdec[:, kb, i_ * 128:(i_ + 1) * 128],
                            rhs=rvb[:, kb, i_ * 128:(i_ + 1) * 128], start=(i_ == 0 and n_ == 0), stop=(i_ == 3 and n_ == 1),
                            skip_group_check=True),
                            reads=[f"Kdec{kb}", f"rvb{kb}"], writes=["ps7"], signal=(i_ == 3 and n_ == 1))
                tr.op("dve", lambda e: e.tensor_tensor(out=Sst[:], in0=Sst[:], in1=Gt[:], op=ALU.mult), reads=["Sbf"], writes=["Sst"])
                for hh in range(2):
                    tr.op("dve", lambda e, hh=hh: e.tensor_tensor(
                        out=Sst[hh * 64:(hh + 1) * 64, :, :], in0=Sst[hh * 64:(hh + 1) * 64, :, :],
                        in1=bk(7)[hh * 64:(hh + 1) * 64, :].rearrange("p (i c) -> p i c", i=4)[:, :, hh * 64:(hh + 1) * 64],
                        op=ALU.add), writes=["ps7", "Sst"])
                tr.op("dve", lambda e: e.tensor_copy(out=Sbf[:], in_=Sst[:]), reads=["Sst"], writes=["Sbf"])
                pump(3)
                tr.op("act", lambda e: e.copy(out=yr, in_=bk(6).rearrange("p (h c) -> p h c", h=8)), writes=["ps6", "pj1"])
                tr.op("dve", lambda e: e.tensor_reduce(out=hst[:, 0, :], in_=yr, axis=AX.X, op=ALU.add), reads=["pj1"], writes=["hst0"])
                tr.op("dve", lambda e: e.tensor_tensor(out=ysq, in0=yr, in1=yr, op=ALU.mult), reads=["pj1"], writes=["pj0"])
                tr.op("dve", lambda e: e.tensor_reduce(out=hst[:, 1, :], in_=ysq, axis=AX.X, op=ALU.add), reads=["pj0"], writes=["hst1"])
                tr.op("dve", lambda e: e.tensor_scalar(out=hst[:, 2, :], in0=hst[:, 0, :], scalar1=1.0 / 64.0, scalar2=None, op0=ALU.mult),
                      reads=["hst0"], writes=["hst2"])
                tr.op("dve", lambda e: e.tensor_tensor(out=hst[:, 0, :], in0=hst[:, 2, :], in1=hst[:, 2, :], op=ALU.mult),
                      reads=["hst2"], writes=["hst0"])
                tr.op("dve", lambda e: e.scalar_tensor_tensor(out=hst[:, 3, :], in0=hst[:, 1, :], scalar=1.0 / 64.0, in1=hst[:, 0, :],
                                                              op0=ALU.mult, op1=ALU.subtract), reads=["hst0", "hst1"], writes=["hst3"])
                tr.op("dve", lambda e: e.tensor_scalar(out=hst[:, 3, :], in0=hst[:, 3, :], scalar1=float(NORM_EPS), scalar2=None, op0=ALU.add),
                      writes=["hst3"])
                tr.op("pool", lambda e: e.tensor_tensor(out=hst[:, 3, :], in0=hst[:, 3, :], in1=neghalf[:], op=ALU.pow), writes=["hst3"])
                tr.op("dve", lambda e: e.tensor_tensor(out=yr, in0=yr, in1=hst[:, 2, :].unsqueeze(2).to_broadcast([128, 8, 64]),
                                                       op=ALU.subtract), reads=["hst2"], writes=["pj1"])
                tr.op("dve", lambda e: e.tensor_tensor(out=yr, in0=yr, in1=hst[:, 3, :].unsqueeze(2).to_broadcast([128, 8, 64]),
                                                       op=ALU.mult), reads=["hst3"], writes=["pj1"])
                tr.op("dve", lambda e: e.tensor_tensor(out=yr, in0=yr, in1=nw[:, 0, :].rearrange("p (h c) -> p h c", h=8),
                                                        op=ALU.mult), writes=["pj1"])
                tr.op("dve", lambda e, qi=qi: e.tensor_tensor(out=merged[:, qi, 0:512], in0=yr.rearrange("p h c -> p (h c)"),
                                                               in1=sgr[:, qi, :], op=ALU.mult), reads=["pj1", f"sgr{qi}"], writes=[f"mgr{qi}"])

            pump(2)
            if state["gu_half"] == 1:
                pump(1)
            state["gu_mode"] = "D"
            if before_D is not None:
                rot_sin()
            obank = {(0, 0): (4, 0), (0, 1): (4, 256), (1, 0): (5, 0), (1, 1): (5, 256)}
            gen_live = [inter_gen is not None]
            for h in range(4):
                started = set()
                steps = []
                for c in range(t):
                    steps.append(("hist", c, (0, 1)))
                    steps.append(("hist", c, (2, 3)))
                steps.append(("cur", t, (0, 1)))
                steps.append(("cur", t, (2, 3)))
                cur_k = (dkT[:, h, :], [f"dkT{i}" for i in range(4)])
                cur_v = (Vaug[:, :, h, :], [f"Vaug{i}" for i in range(4)])
                slot_of = {}
                info = {}

                def do_qk(si):
                    kind, c, kbs = steps[si]
                    if kind == "hist":
                        if c not in slot_of:
                            sl = state["kv"] % 2
                            state["kv"] += 1
                            slot_of[c] = sl
                            tr.dma(out=KTp[sl][:], in_=KTd[h, :, c * 512:(c + 1) * 512], reads=[f"kvd{c}"], writes=[f"KTp{sl}"], sem=f"KTp{sl}")
                            tr.dma(out=Vp[sl][:], in_=Vd[h, :, 4 * c:4 * c + 4, :], reads=[f"kvd{c}"], writes=[f"Vp{sl}"], sem=f"Vp{sl}")
                        sl = slot_of[c]
                        kt_ap, kres = KTp[sl], [f"KTp{sl}"]
                        vt = (Vp[sl], [f"Vp{sl}"])
                    else:
                        kt_ap, kres = cur_k
                        vt = cur_v
                    if gen_live[0]:
                        b = (2, 3)
                    else:
                        b = (0, 1) if state["lastS"] == (2, 3) else (2, 3)
                    state["lastS"] = b
                    only_b = (kind == "cur" and kbs == (2, 3))
                    for j, kb in enumerate(kbs):
                        for sub in range(2):
                            if only_b:
                                o_ = bk(b[sub])[:, j * 256 + 128:(j + 1) * 256]
                                r_ = dqT[sub * 64:(sub + 1) * 64, h, 128:256]
                                rres = ["dqT1"]
                            else:
                                o_ = bk(b[sub])[:, j * 256:(j + 1) * 256]
                                r_ = dqT[sub * 64:(sub + 1) * 64, h, :]
                                rres = ["dqT0", "dqT1"]
                            tr.op("pe", lambda e, o_=o_, r_=r_, sub=sub, kb=kb, kt_ap=kt_ap: e.matmul(
                                o_, lhsT=kt_ap[sub * 64:(sub + 1) * 64, kb * 128:(kb + 1) * 128], rhs=r_, start=True, stop=True),
                                reads=kres + rres, writes=[f"ps{b[sub]}"], signal=(j == 1))
                    info[si] = (b, vt)

                def ex(b, p, sub, c0, c1, special):
                    kw = dict(bias=sbias[:, 0:1]) if special else {}
                    tr.op("act", lambda e: e.activation(out=PT[p][:, sub * 512 + c0:sub * 512 + c1], in_=bk(b[sub])[:, c0:c1],
                                                        func=AF.Exp, scale=0.125, **kw),
                          writes=[f"ps{b[sub]}", f"PT{p}"])

                def do_exp(si):
                    kind, c, kbs = steps[si]
                    b, (v_ap, vres) = info[si]
                    p = state["pt"] % 2
                    state["pt"] += 1
                    ptv = PT[p][:].rearrange("p (s q) -> p s q", s=2)
                    if kind == "hist":
                        for sub in range(2):
                            ex(b, p, sub, 0, 512, False)
                        plan = [(0, kbs[0], [0, 1], []), (1, kbs[1], [0, 1], [])]
                    elif kbs == (0, 1):
                        for sub in range(2):
                            ex(b, p, sub, 0, 256, False)
                            ex(b, p, sub, 256, 384, True)
                            ex(b, p, sub, 384, 512, False)
                        tr.op("dve", lambda e: e.tensor_tensor(out=ptv[:, :, 0:128], in0=ptv[:, :, 0:128],
                                                               in1=tri[:].unsqueeze(1).to_broadcast([128, 2, 128]), op=ALU.mult),
                              writes=[f"PT{p}"])
                        plan = [(0, 0, [0, 1], []), (1, 1, [0, 1], [0])]
                    else:
                        for sub in range(2):
                            ex(b, p, sub, 128, 256, False)
                            ex(b, p, sub, 384, 512, True)
                        tr.op("dve", lambda e: e.tensor_tensor(out=ptv[:, :, 128:256], in0=ptv[:, :, 128:256],
                                                               in1=tri[:].unsqueeze(1).to_broadcast([128, 2, 128]), op=ALU.mult),
                              writes=[f"PT{p}"])
                        plan = [(0, 2, [1], []), (1, 3, [1], [1])]
                    info[si] = (b, (v_ap, vres), p, plan)

                def do_pv(si):
                    b, (v_ap, vres), p, plan = info[si]
                    for (j, kb, qis, last_for) in plan:
                        for sub in range(2):
                            for qi in qis:
                                ob__, oc0 = obank[(sub, qi)]
                                st = ob__ not in started
                                started.add(ob__)
                                c0 = sub * 512 + j * 256 + qi * 128
                                tr.op("pe", lambda e, ob__=ob__, oc0=oc0, c0=c0, st=st, kb=kb: e.matmul(
                                    bk(ob__)[:, oc0:oc0 + 129], lhsT=PT[p][:, c0:c0 + 128], rhs=v_ap[:, kb, 0:129],
                                    start=st, stop=False, skip_group_check=True),
                                    reads=[f"PT{p}"] + vres, writes=[f"ps{ob__}"], signal=(sub == 1 and qi == qis[-1]))

                qk_done = set()
                for si in range(len(steps)):
                    if si not in qk_done:
                        do_qk(si)
                        qk_done.add(si)
                    if gen_live[0]:
                        do_exp(si)
                        if next(inter_gen, "done") == "done":
                            gen_live[0] = False
                        do_pv(si)
                    else:
                        if si + 1 < len(steps):
                            do_qk(si + 1)
                            qk_done.add(si + 1)
                        do_exp(si)
                        do_pv(si)
                for qi in range(2):
                    for sub in range(2):
                        ob__, oc0 = obank[(sub, qi)]
                        tr.op("act", lambda e, ob__=ob__, oc0=oc0, sub=sub: e.copy(out=ob[:, sub, 0:129], in_=bk(ob__)[:, oc0:oc0 + 129]),
                              writes=[f"ps{ob__}", "ob"])
                    tr.op("dve", lambda e: e.reciprocal(out=dst_[:, 0:2], in_=ob[:, :, 128]), reads=["ob"], writes=["dst"])
                    tr.op("dve", lambda e: e.tensor_tensor(out=dst_[:, 2:3], in0=dst_[:, 1:2], in1=neglam[:], op=ALU.mult), writes=["dst"])
                    tr.op("dve", lambda e: e.tensor_scalar(out=dfa, in0=ob[:, 0, 0:128], scalar1=dst_[:, 0:1], scalar2=None, op0=ALU.mult),
                          reads=["ob", "dst"], writes=["pj2"])
                    tr.op("dve", lambda e: e.scalar_tensor_tensor(out=dfa, in0=ob[:, 1, 0:128], scalar=dst_[:, 2:3], in1=dfa,
                                                                  op0=ALU.mult, op1=ALU.add), reads=["ob", "dst"], writes=["pj2"])
                    tr.op("dve", lambda e: e.tensor_tensor(out=dfb, in0=dfa, in1=dfa, op=ALU.mult), reads=["pj2"], writes=["pj2"])
                    tr.op("dve", lambda e: e.tensor_reduce(out=dst_[:, 3:4], in_=dfb, axis=AX.X, op=ALU.add), reads=["pj2"], writes=["dst"])
                    tr.op("dve", lambda e: e.tensor_scalar(out=dst_[:, 3:4], in0=dst_[:, 3:4], scalar1=1.0 / 128.0, scalar2=float(NORM_EPS),
                                                           op0=ALU.mult, op1=ALU.add), writes=["dst"])
                    tr.op("pool", lambda e: e.tensor_tensor(out=dst_[:, 3:4], in0=dst_[:, 3:4], in1=neghalf[:, 0:1], op=ALU.pow), writes=["dst"])
                    tr.op("dve", lambda e: e.tensor_scalar(out=dfa, in0=dfa, scalar1=dst_[:, 3:4], scalar2=float(1.0 - LAMBDA_INIT),
                                                           op0=ALU.mult, op1=ALU.mult), reads=["dst"], writes=["pj2"])
                    tr.op("dve", lambda e, qi=qi, h=h: e.tensor_tensor(out=merged[:, qi, 512 + h * 128:512 + (h + 1) * 128], in0=dfa,
                                                                        in1=nw[:, 1, h * 128:(h + 1) * 128], op=ALU.mult),
                          reads=["pj2"], writes=[f"mgd{qi}_{h}"])

            if inter_gen is not None:
                for _ in inter_gen:
                    pass
            for qi in range(2):
                transpose_blocks(lambda j, qi=qi: merged[:, qi, j * 128:(j + 1) * 128], 8, xTB[:, :, qi * 128:(qi + 1) * 128],
                                 [f"mgr{qi}"] + [f"mgd{qi}_{h}" for h in range(4)], [f"xTB{qi}"], evac="act")
            for half in range(2):
                s = need("g", "wout", half * 512)
                for qi, blk in enumerate(own_blks):
                    b = 4 + half * 2 + qi
                    for mc in range(8):
                        tr.op("pe", lambda e, b=b, mc=mc, qi=qi: e.matmul(bk(b), lhsT=xTB[:, mc, qi * 128:(qi + 1) * 128], rhs=gview(s)[:, mc, :],
                                                                          start=(mc == 0), stop=(mc == 7)),
                              reads=[f"wb{s}", f"xTB{qi}"], writes=[f"ps{b}"], signal=(mc == 7))
                    dst = xf[:, sl4[blk], half * 512:(half + 1) * 512]
                    tr.op("dve", lambda e, b=b, dst=dst: e.tensor_tensor(out=dst, in0=bk(b), in1=dst, op=ALU.add),
                          writes=[f"ps{b}", f"xf{sl4[blk]}"])
                release(s)
            for blk in own_blks:
                layer_norm(sl4[blk], 2, 3)

        def stage_F(t, between=None):
            sl4 = slots(t)
            ffn("wg2", "wu2", "wd2", 256, [(sl4[blk], qi * 128) for qi, blk in enumerate(own_blks)], xTB, ["xTB0", "xTB1"],
                between=between)
            for qi, blk in enumerate(own_blks):
                layer_norm(sl4[blk], 4, 5)
                g = 2 * t + qi
                tr.dma(out=out_d[g * 128:(g + 1) * 128, :], in_=xf[:, sl4[blk], :], reads=[f"xf{sl4[blk]}"], writes=[],
                       sem=f"out{sl4[blk]}")

        load_x(0)
        prep_A(0)
        for _ in ffn_A_gu(0):
            pass
        ffn_A_down(0)
        drain_casts(len(pending_casts))
        x1_to_T(0)
        for t in range(nt):
            last = (t + 1 == nt)
            if last:
                stage_BE(t)
                x2_to_T(t)
                stage_F(t)
            else:
                stage_BE(t, before_D=lambda t=t: rot_tables(t + 1), inter_gen=ffn_A_gu(t + 1, inter=True))
                ffn_A_down(t + 1, between=lambda t=t: x2_to_T(t))
                stage_F(t, between=lambda t=t: x1_to_T(t + 1))

        for key in [k for k in tr.semh if k.startswith("dma:out")]:
            nc.sync.wait_ge(tr.semh[key], tr.cnt[key])
        if not recording:
            assert ring_state["consumed"] == len(plan)
            print("instructions:", tr.n_inst)
    if recording:
        return build_program(nt=nt, stop=stop, plan_in=plan)
    return nc


_NC_CACHE = {}


def _tables(p):
    h = np.arange(8, dtype=np.float64)
    gam = 1.0 - 2.0 ** (-5.0 - h)
    k = np.arange(128)[:, None]
    q = np.arange(128)[None, :]
    rel = (q - k).astype(np.float64)
    DoT = np.zeros((128, 8, 128), np.float64)
    DxT = np.zeros((128, 8, 128), np.float64)
    for hd in range(8):
        sl = (hd % 2) * 4 + hd // 2
        DoT[:, sl, :] = np.where(rel >= 0, 0.125 * gam[hd] ** np.maximum(rel, 0.0), 0.0)
        if p == 1:
            DxT[:, sl, :] = 0.125 * gam[hd] ** (128.0 + rel)
    qdec = np.zeros((128, 4, 128), np.float64)
    Gt = np.zeros((128, 4, 64), np.float64)
    for i in range(4):
        for hh in range(2):
            g_ = gam[2 * i + hh]
            qdec[hh * 64:(hh + 1) * 64, i, :] = (g_ ** (np.arange(128) + 1.0 + 128.0 * p))[None, :]
            Gt[hh * 64:(hh + 1) * 64, i, :] = g_ ** 256.0
    kdec = np.zeros((128, 2, 8), np.float64)
    tt = np.arange(128, dtype=np.float64)
    for hh in range(8):
        kdec[:, 0, hh] = 0.125 * gam[hh] ** (255.0 - (tt + 128.0 * p))
        kdec[:, 1, hh] = 0.125 * gam[hh] ** (255.0 - (tt + 128.0 * (1 - p)))
    tri = (np.arange(128)[:, None] <= np.arange(128)[None, :]).astype(np.float32)
    inv_ret = 1.0 / (10000.0 ** np.linspace(0.0, 1.0, 32, dtype=np.float32))
    inv_rope = 1.0 / (10000.0 ** (np.arange(0, 64, 2, dtype=np.float32) / 64.0))
    inv = np.concatenate([inv_ret.astype(np.float32), inv_rope.astype(np.float32)])[None, :].repeat(128, 0)
    sb = np.full((128, 1), 0.0 if p == 1 else NEG_BIG, np.float32)
    f = lambda a: np.ascontiguousarray(a.astype(np.float32))
    return dict(DoT=f(DoT), DxT=f(DxT), qdec=f(qdec), Gt=f(Gt), kdec=f(kdec), tri=f(tri),
                ident=np.eye(128, dtype=np.float32), inv=f(inv), sbias=sb)


def kernel(x, positions, ffn1_w_gate, ffn1_w_up, ffn1_w_down, ln1_w, ln1_b,
           w_in, ret_norm_w, diff_lambda_q1, diff_lambda_k1, diff_lambda_q2, diff_lambda_k2,
           diff_norm_w, w_out, ln2_w, ln2_b, ffn2_w_gate, ffn2_w_up, ffn2_w_down, ln3_w, ln3_b):
    if "nc" not in _NC_CACHE:
        _NC_CACHE["nc"] = build_program()
    nc = _NC_CACHE["nc"]
    A = lambda a: np.ascontiguousarray(np.asarray(a))
    x = A(x).astype(np.float32, copy=False)
    positions = A(positions).astype(np.int32, copy=False)
    rep = lambda v: np.ascontiguousarray(np.broadcast_to(np.asarray(v, np.float32).reshape(1, -1), (128, np.asarray(v).size)))
    lnv = np.ascontiguousarray(np.stack([rep(ln1_w), rep(ln1_b), rep(ln2_w), rep(ln2_b), rep(ln3_w), rep(ln3_b)], axis=1))
    nwv = np.ascontiguousarray(np.stack([rep(ret_norm_w), rep(diff_norm_w)], axis=1))
    lamv = np.ascontiguousarray(np.stack([rep(diff_lambda_q1), rep(diff_lambda_k1), rep(diff_lambda_q2), rep(diff_lambda_k2)], axis=1))
    shared = {
        "wg1": A(ffn1_w_gate)[0], "wu1": A(ffn1_w_up)[0], "wd1": A(ffn1_w_down)[0],
        "win": A(w_in)[0], "wout": A(w_out)[0],
        "wg2": A(ffn2_w_gate)[0], "wu2": A(ffn2_w_up)[0], "wd2": A(ffn2_w_down)[0],
        "lnv": lnv, "nw": nwv, "lam": lamv,
    }
    shared = {k: np.ascontiguousarray(v, dtype=np.float32) for k, v in shared.items()}
    tabs = [_tables(0), _tables(1)]
    in_maps = []
    for c in range(8):
        b, p = c // 2, c % 2
        order = []
        for g in range(NBLK // 2):
            order += [2 * g + p, 2 * g + 1 - p]
        xb = x[b].reshape(NBLK, 128, D)[order].reshape(SEQ, D)
        pb = positions[b].reshape(NBLK, 128)[order]
        m = dict(shared)
        m["x"] = np.ascontiguousarray(xb)
        m["pos"] = np.ascontiguousarray(pb.T)
        m.update(tabs[p])
        in_maps.append(m)
    res = run_bass_kernel_spmd(nc, in_maps, core_ids=list(range(8)))
    out = np.empty((BATCH, SEQ, D), np.float32)
    for c in range(8):
        b, p = c // 2, c % 2
        o = np.asarray(res.results[c]["out"]).reshape(NBLK // 2, 128, D)
        ov = out[b].reshape(NBLK, 128, D)
        for g in range(NBLK // 2):
            ov[2 * g + p] = o[g]
    return out
```

```python
import math
import contextlib
import numpy as np
import concourse.bass as bass
import concourse.mybir as mybir
from concourse.bass_utils import run_bass_kernel_spmd

F32 = mybir.dt.float32
BF16 = mybir.dt.bfloat16
I32 = mybir.dt.int32
AF = mybir.ActivationFunctionType
ALU = mybir.AluOpType
AX = mybir.AxisListType

D = 1024
FF = 2816
SEQ = 8192
BATCH = 4
NBLK = SEQ // 128
NT = 16
INW = 3584
ALPHA = (2.0) ** 0.25
LAMBDA_INIT = 0.8 - 0.6 * math.exp(0.0)
LN_EPS = 1e-5
NORM_EPS = 1e-6
NEG_BIG = -30000.0
NFC = FF // 128


class Tracker:
    def __init__(self, nc, es):
        self.nc = nc
        self.es = es
        self.eng = {"pe": nc.tensor, "act": nc.scalar, "dve": nc.vector, "pool": nc.gpsimd, "sp": nc.sync}
        self.semh = {}
        self.cnt = {}
        for k in ["pe", "act", "dve", "pool"]:
            self.semh[k] = es.enter_context(nc.semaphore("s_" + k))
            self.cnt[k] = 0
        self.last_w = {}
        self.readers = {}
        self.waited = {e: {} for e in self.eng}
        self.n_inst = 0

    def dsem(self, name):
        key = "dma:" + name
        if key not in self.semh:
            self.semh[key] = self.es.enter_context(self.nc.semaphore("d_" + name))
            self.cnt[key] = 0
        return key

    def _wait(self, e, dep):
        key, val = dep
        if e == "pe" and key == "pe":
            return
        if self.waited[e].get(key, 0) >= val:
            return
        if key == "pe":
            assert val <= self.cnt["pe"], "dependency on unsignaled PE op"
        self.eng[e].wait_ge(self.semh[key], val)
        self.waited[e][key] = val

    def _deps(self, e, reads, writes):
        deps = []
        for r in reads:
            if r in self.last_w:
                deps.append(self.last_w[r])
        for w in writes:
            if w in self.last_w:
                deps.append(self.last_w[w])
            deps.extend(self.readers.get(w, ()))
        for d in deps:
            self._wait(e, d)

    def _record(self, me, reads, writes):
        for r in reads:
            self.readers.setdefault(r, []).append(me)
        for w in writes:
            self.last_w[w] = me
            self.readers[w] = []

    def op(self, e, fn, reads=(), writes=(), signal=True):
        self._deps(e, reads, writes)
        inst = fn(self.eng[e])
        self.n_inst += 1
        if e == "pe" and not signal:
            me = ("pe", self.cnt["pe"] + 1)
        else:
            self.cnt[e] += 1
            inst.then_inc(self.semh[e], 1)
            me = (e, self.cnt[e])
        self._record(me, reads, writes)

    def dma(self, out, in_, reads=(), writes=(), sem=None, q="sp"):
        key = self.dsem(sem)
        self._deps(q, reads, writes)
        inst = self.eng[q].dma_start(out=out, in_=in_)
        self.cnt[key] += 16
        inst.then_inc(self.semh[key], 16)
        self.n_inst += 1
        me = (key, self.cnt[key])
        self._record(me, reads, writes)

    def barrier_all(self, res):
        dep = self.last_w[res]
        for e in self.eng:
            self._wait(e, dep)


def build_program(nt=NT, stop=None, plan_in=None):
    nc = bass.Bass("TRN2", target_bir_lowering=False)

    def din(name, shape, dt=F32):
        return nc.dram_tensor(name, list(shape), dt, kind="ExternalInput").ap()

    def dint(name, shape, dt=BF16):
        return nc.dram_tensor(name, list(shape), dt, kind="Internal").ap()

    x_d = din("x", [SEQ, D])
    pos_d = din("pos", [128, NBLK], I32)
    wsrc = {
        "wg1": din("wg1", [D, FF]), "wu1": din("wu1", [D, FF]), "wd1": din("wd1", [FF, D]),
        "win": din("win", [D, INW]), "wout": din("wout", [D, D]),
        "wg2": din("wg2", [D, FF]), "wu2": din("wu2", [D, FF]), "wd2": din("wd2", [FF, D]),
    }
    lnv_d = din("lnv", [128, 6, D])
    nw_d = din("nw", [128, 2, 512])
    lam_d = din("lam", [128, 4, 64])
    DoT_d = din("DoT", [128, 8, 128])
    DxT_d = din("DxT", [128, 8, 128])
    qdec_d = din("qdec", [128, 4, 128])
    G_d = din("Gt", [128, 4, 64])
    kdec_d = din("kdec", [128, 2, 8])
    tri_d = din("tri", [128, 128])
    ident_d = din("ident", [128, 128])
    inv_d = din("inv", [128, 64])
    sbias_d = din("sbias", [128, 1])
    out_d = nc.dram_tensor("out", [SEQ // 2, D], F32, kind="ExternalOutput").ap()

    wb = {k: dint(k + "b", v.shape) for k, v in wsrc.items()}
    KTd = dint("KTd", [4, 128, SEQ])
    Vd = dint("Vd", [4, 128, NBLK, 130])

    es = contextlib.ExitStack()
    with es:
        def sb(name, shape, dt=F32):
            return es.enter_context(nc.sbuf_tensor("sb_" + name, list(shape), dt))

        tr = Tracker(nc, es)

        xf = sb("xf", [128, 6, D])
        xbf = [sb(f"xbf{i}", [128, D], BF16) for i in range(2)]
        xTA = sb("xTA", [128, 8, 512], BF16)
        xTB = sb("xTB", [128, 8, 512], BF16)
        hT = sb("hT", [128, NFC, 512], BF16)
        sg = [sb(f"sg{i}", [128, 512]) for i in range(2)]
        wbuf = [sb(f"wbuf{i}", [128, 4096], BF16) for i in range(4)]
        lnv = sb("lnv", [128, 6, D])
        nw = sb("nw", [128, 2, 512])
        DoT = sb("DoT", [128, 8, 128])
        DxT = sb("DxT", [128, 8, 128])
        qdec = sb("qdec", [128, 4, 128])
        Gt = sb("Gt", [128, 4, 64])
        kdec = sb("kdec", [128, 2, 8])
        tri = sb("tri", [128, 128], BF16)
        ident = sb("ident", [128, 128], BF16)
        inv = sb("inv", [128, 64])
        sbias = sb("sbias", [128, 1])
        posi = sb("posi", [128, NBLK], I32)
        posf = sb("posf", [128, NBLK])
        neghalf = sb("neghalf", [128, 8])
        neglam = sb("neglam", [128, 1])
        lams = sb("lams", [128, 4])
        sn = sb("sn", [128, 4, 64])
        cs = sb("cs", [128, 4, 64])
        bst = sb("bst", [128, 2, 6])
        mv = sb("mv", [128, 2])
        rstd = sb("rstd", [128, 1])
        pj = [sb(f"pj{i}", [128, 512]) for i in range(3)]
        _v = lambda tns, a, b: tns[:, a:b].rearrange("p (x y) -> p x y", x=4)
        ang, angk = _v(pj[0], 0, 256), _v(pj[0], 256, 512)
        angi_t = sb("angi", [128, 4, 64], I32)
        angi, angm = angi_t[:], _v(pj[1], 256, 512)
        rsn, rcs = _v(pj[2], 0, 256), _v(pj[2], 256, 512)
        ysq = pj[0][:].rearrange("p (h c) -> p h c", h=8)
        yr = pj[1][:].rearrange("p (h c) -> p h c", h=8)
        dfa, dfb = pj[2][:, 0:128], pj[2][:, 128:256]
        rt = [sb(f"rt{i}", [128, 256]) for i in range(4)]
        rot = [sb(f"rot{i}", [128, 512], BF16) for i in range(2)]
        rvb = sb("rvb", [128, 4, 512], BF16)
        Vaug = sb("Vaug", [128, 4, 4, 130], BF16)
        sgr = sb("sgr", [128, 2, 512])
        Kdec = sb("Kdec", [128, 4, 512], BF16)
        rkT = sb("rkT", [128, 4, 512], BF16)
        dkT = sb("dkT", [128, 4, 512], BF16)
        rqT = sb("rqT", [128, 4, 256], BF16)
        rqTd = sb("rqTd", [128, 4, 256], BF16)
        dqT = sb("dqT", [128, 4, 256], BF16)
        AoT = sb("AoT", [128, 8, 128], BF16)
        AxT = sb("AxT", [128, 8, 128], BF16)
        Sst = sb("Sst", [128, 4, 64])
        Sbf = sb("Sbf", [128, 4, 64], BF16)
        hst = sb("hst", [128, 4, 8])
        PT = [sb(f"PT{i}", [128, 1024], BF16) for i in range(2)]
        merged = sb("merged", [128, 2, D], BF16)

        banks = [es.enter_context(nc.psum_tensor(f"bank{i}", [128, 512], F32)) for i in range(8)]
        es_init = contextlib.ExitStack()
        sbi = lambda name, shape, dt=F32: es_init.enter_context(nc.sbuf_tensor("sb_" + name, list(shape), dt))
        lam = sbi("lam", [128, 4, 64])
        lamt = sbi("lamt", [128, 4, 64])
        trif = sbi("trif", [128, 128])
        identf = sbi("identf", [128, 128])

        def bk(i):
            return banks[i][:]

        def bkb(i):
            return banks[i][:].bitcast(BF16)

        def cast_pieces(name):
            src = wsrc[name]
            dst = wb[name]
            R, C = src.shape
            k = 1
            while C // k > 2048:
                k *= 2
            sv = src.rearrange("r (k c) -> (r k) c", k=k) if k > 1 else src
            dv = dst.rearrange("r (k c) -> (r k) c", k=k) if k > 1 else dst
            rows = R * k
            step = 512
            chunks = [(r0, min(rows, r0 + step)) for r0 in range(0, rows, step)]
            out = []
            for ci, (r0, r1) in enumerate(chunks):
                def piece(dep=None, r0=r0, r1=r1, lastp=(ci == len(chunks) - 1)):
                    if dep is not None and dep in tr.last_w:
                        tr._wait("pool", tr.last_w[dep])
                    tr.dma(out=dv[r0:r1, :], in_=sv[r0:r1, :], reads=[], writes=[], sem="c_" + name, q="pool")
                    if lastp:
                        tr.last_w["w_" + name] = ("dma:c_" + name, tr.cnt["dma:c_" + name])
                out.append(piece)
            return out

        def cast_weight(name):
            for p_ in cast_pieces(name):
                p_()

        pending_casts = []

        def drain_casts(n, dep=None):
            for _ in range(n):
                if pending_casts:
                    pending_casts.pop(0)(dep)

        consts = [(lnv, lnv_d), (nw, nw_d), (lam, lam_d), (DoT, DoT_d), (DxT, DxT_d), (qdec, qdec_d),
                  (Gt, G_d), (kdec, kdec_d), (trif, tri_d), (identf, ident_d), (inv, inv_d),
                  (sbias, sbias_d), (posi, pos_d)]
        for t_, d_ in consts:
            tr.dma(out=t_[:], in_=d_, writes=["consts"], sem="consts")
        for name in ["wg1", "wu1", "wd1"]:
            cast_weight(name)
        for name in ["win", "wout", "wg2", "wu2", "wd2"]:
            pending_casts.extend(cast_pieces(name))
        tr.barrier_all("consts")
        del tr.last_w["consts"]

        tr.op("dve", lambda e: e.tensor_copy(out=tri[:], in_=trif[:]), writes=["tri"])
        tr.op("dve", lambda e: e.tensor_copy(out=ident[:], in_=identf[:]), writes=["ident"])
        tr.op("dve", lambda e: e.tensor_copy(out=posf[:], in_=posi[:]), writes=["posf"])
        tr.op("dve", lambda e: e.memset(neghalf[:], -0.5), writes=["neghalf"])
        tr.op("dve", lambda e: e.memset(Vaug[:].rearrange("p a b c -> p (a b) c")[:, :, 128:130], 1.0), writes=["Vaug"])
        tr.op("dve", lambda e: e.memset(Sst[:], 0.0), writes=["Sst"])
        tr.op("dve", lambda e: e.memset(Sbf[:], 0.0), writes=["Sbf"])
        tr.op("dve", lambda e: e.tensor_tensor(out=lamt[:, 0, :], in0=lam[:, 0, :], in1=lam[:, 1, :], op=ALU.mult), writes=["lamt"])
        tr.op("dve", lambda e: e.tensor_tensor(out=lamt[:, 1, :], in0=lam[:, 2, :], in1=lam[:, 3, :], op=ALU.mult), writes=["lamt"])
        tr.op("dve", lambda e: e.tensor_reduce(out=lams[:, 0:2], in_=lamt[:, 0:2, :], axis=AX.X, op=ALU.add), reads=["lamt"], writes=["lams"])
        tr.op("act", lambda e: e.activation(out=lams[:, 2:4], in_=lams[:, 0:2], func=AF.Exp), reads=[], writes=["lams"])
        tr.op("dve", lambda e: e.tensor_tensor(out=neglam[:], in0=lams[:, 3:4], in1=lams[:, 2:3], op=ALU.subtract), reads=["lams"], writes=["neglam"])
        tr.op("dve", lambda e: e.tensor_scalar(out=neglam[:], in0=neglam[:], scalar1=-float(LAMBDA_INIT), scalar2=None, op0=ALU.add), writes=["neglam"])
        for r_ in ["tri", "ident", "posf", "neghalf", "neglam", "Vaug"]:
            tr.barrier_all(r_)
            del tr.last_w[r_]
            tr.readers.pop(r_, None)

        for r_ in ["lams", "lamt"]:
            tr.barrier_all(r_)
        es_init.close()
        KTp = [sb(f"KTp{i}", [128, 512], BF16) for i in range(2)]
        Vp = [sb(f"Vp{i}", [128, 4, 130], BF16) for i in range(2)]
        ob = sb("ob", [128, 2, 130])
        dst_ = sb("dst", [128, 8])

        state = {"gu_mode": "D", "gu_half": 0, "lastS": (0, 1), "rot": 0, "tb": 0, "slab": 0, "gu": 0, "pjs": 0, "wbank": 0, "sbank": 0, "pt": 0, "kv": 0, "xbf": 0}

        CG_ORDER = [1, 2, 5, 6, 0, 3, 4]
        recording = plan_in is None
        plan = [] if recording else list(plan_in)
        ring_state = {"issued": 0, "consumed": 0, "done": 0}

        def gview(i):
            return wbuf[i][:].rearrange("p (dc c) -> p dc c", dc=8)

        def dview(i):
            return wbuf[i][:].rearrange("p (fc c) -> p fc c", fc=4)

        def _issue(k):
            kind, wn, c0, cw = plan[k]
            i = k % 4
            if kind == "g":
                tr.dma(out=gview(i)[:, :, :cw], in_=wb[wn][:, c0:c0 + cw].rearrange("(dc p) c -> p dc c", p=128),
                       reads=["w_" + wn], writes=[f"wb{i}"], sem=f"wb{i}")
            else:
                tr.dma(out=dview(i)[:, :cw // 128, :], in_=wb[wn][c0:c0 + cw, :].rearrange("(fc p) c -> p fc c", p=128),
                       reads=["w_" + wn], writes=[f"wb{i}"], sem=f"wb{i}")

        def need(kind, wn, c0, cw=512):
            k = ring_state["consumed"]
            if recording:
                plan.append((kind, wn, c0, cw))
            assert plan[k] == (kind, wn, c0, cw), (plan[k], kind, wn, c0, cw)
            ring_state["consumed"] += 1
            _pump()
            assert ring_state["issued"] > k, "weight-slab ring exhausted: more than 4 slabs live"
            live_k[k % 4] = k
            return k % 4

        def _pump():
            while ring_state["issued"] < min(len(plan), ring_state["done"] + 4):
                _issue(ring_state["issued"])
                ring_state["issued"] += 1

        live_k = {}
        done_set = set()

        def release(*bufs):
            for b_ in bufs:
                done_set.add(live_k[b_])
            while ring_state["done"] in done_set:
                done_set.discard(ring_state["done"])
                ring_state["done"] += 1
            _pump()

        def transpose_blocks(src_ap_fn, n, dst_ap, src_res, dst_res, evac="act"):
            b = state["tb"] % 2
            state["tb"] += 1
            pv = bkb(b)
            for j in range(n):
                tr.op("pe", lambda e, j=j: e.transpose(out=pv[:, j * 128:(j + 1) * 128], in_=src_ap_fn(j), identity=ident[:]),
                      reads=src_res, writes=[f"ps{b}"], signal=(j == n - 1))
            src = pv[:, 0:n * 128].rearrange("p (a b) -> p a b", a=n)
            if evac == "act":
                tr.op("act", lambda e: e.copy(out=dst_ap, in_=src), writes=[f"ps{b}"] + dst_res)
            else:
                tr.op("dve", lambda e: e.tensor_copy(out=dst_ap, in_=src), writes=[f"ps{b}"] + dst_res)

        def stream_to_T(slot, dstT, dpre, col0, scale_alpha, light_act=False):
            s = state["xbf"] % 2
            state["xbf"] += 1
            if light_act:
                tr.op("dve", lambda e: e.tensor_copy(out=xbf[s][:], in_=xf[:, slot, :]), reads=[f"xf{slot}"], writes=[f"xbf{s}"])
            else:
                tr.op("act", lambda e: e.copy(out=xbf[s][:], in_=xf[:, slot, :]), reads=[f"xf{slot}"], writes=[f"xbf{s}"])
            ev = "dve"
            transpose_blocks(lambda j: xbf[s][:, j * 128:(j + 1) * 128], 8, dstT[:, :, col0:col0 + 128],
                             [f"xbf{s}"], [f"{dpre}{col0 // 128}"], evac=ev)
            if scale_alpha:
                if light_act:
                    tr.op("dve", lambda e: e.tensor_scalar(out=xf[:, slot, :], in0=xf[:, slot, :], scalar1=float(ALPHA), scalar2=None,
                                                            op0=ALU.mult), writes=[f"xf{slot}"])
                else:
                    tr.op("act", lambda e: e.mul(out=xf[:, slot, :], in_=xf[:, slot, :], mul=float(ALPHA)), writes=[f"xf{slot}"])

        def layer_norm(blk, iw, ib):
            src = xf[:, blk, :]
            res = f"xf{blk}"
            tr.op("dve", lambda e: e.bn_stats(out=bst[:, 0, :], in_=xf[:, blk, 0:512]), reads=[res], writes=["bst"])
            tr.op("dve", lambda e: e.bn_stats(out=bst[:, 1, :], in_=xf[:, blk, 512:1024]), reads=[res], writes=["bst"])
            tr.op("dve", lambda e: e.bn_aggr(out=mv[:], in_=bst[:].rearrange("p a b -> p (a b)")), reads=["bst"], writes=["mv"])
            tr.op("dve", lambda e: e.tensor_scalar(out=rstd[:], in0=mv[:, 1:2], scalar1=float(LN_EPS), scalar2=None, op0=ALU.add),
                  reads=["mv"], writes=["rstd"])
            tr.op("pool", lambda e: e.tensor_tensor(out=rstd[:], in0=rstd[:], in1=neghalf[:, 0:1], op=ALU.pow), writes=["rstd"])
            tr.op("dve", lambda e: e.scalar_tensor_tensor(out=src, in0=src, scalar=mv[:, 0:1], in1=lnv[:, iw, :],
                                                          op0=ALU.subtract, op1=ALU.mult), reads=["mv"], writes=[res])
            tr.op("dve", lambda e: e.scalar_tensor_tensor(out=src, in0=src, scalar=rstd[:, 0:1], in1=lnv[:, ib, :],
                                                          op0=ALU.mult, op1=ALU.add), reads=["rstd"], writes=[res])

        def ffn_gu(wn_g, wn_u, ntok, srcT, src_res, inter=False):
            nslab = 6
            for j in range(nslab):
                cw = 512 if j < 5 else 256
                ig = need("g", wn_g, j * 512, cw)
                iu = need("g", wn_u, j * 512, cw)
                wgv, wuv = gview(ig), gview(iu)
                for k in range(cw // 128):
                    fc = j * 4 + k
                    q = state["gu"] % 2
                    state["gu"] += 1
                    mode = state["gu_mode"] if inter else "N"
                    if mode == "D":
                        bg, bu = (6, 7) if q == 0 else (0, 1)
                    elif mode == "B":
                        bg, bu = 6, 7
                    elif mode == "C":
                        bg, bu = 0, 1
                    else:
                        bg, bu = (2, 3) if q == 0 else (4, 5)
                    state["gu_half"] = 1
                    for dc in range(8):
                        tr.op("pe", lambda e, dc=dc: e.matmul(bk(bg)[:, :ntok], lhsT=wgv[:, dc, k * 128:(k + 1) * 128],
                                                              rhs=srcT[:, dc, :ntok], start=(dc == 0), stop=(dc == 7)),
                              reads=[f"wb{ig}"] + src_res, writes=[f"ps{bg}"], signal=(dc == 7))
                    if inter:
                        yield
                    for dc in range(8):
                        tr.op("pe", lambda e, dc=dc: e.matmul(bk(bu)[:, :ntok], lhsT=wuv[:, dc, k * 128:(k + 1) * 128],
                                                              rhs=srcT[:, dc, :ntok], start=(dc == 0), stop=(dc == 7)),
                              reads=[f"wb{iu}"] + src_res, writes=[f"ps{bu}"], signal=(dc == 7))
                    sq = sg[q][:, :ntok]
                    if mode == "D":
                        tr.op("act", lambda e: e.activation(out=sq, in_=bk(bg)[:, :ntok], func=AF.Tanh, scale=0.5),
                              reads=[f"ps{bg}"], writes=[f"sg{q}"])
                        tr.op("dve", lambda e: e.scalar_tensor_tensor(out=sq, in0=sq, scalar=1.0, in1=bk(bg)[:, :ntok],
                                                                      op0=ALU.add, op1=ALU.mult), writes=[f"ps{bg}", f"sg{q}"])
                        tr.op("dve", lambda e: e.scalar_tensor_tensor(out=hT[:, fc, :ntok], in0=sq, scalar=0.5, in1=bk(bu)[:, :ntok],
                                                                      op0=ALU.mult, op1=ALU.mult),
                              reads=[f"sg{q}"], writes=[f"ps{bu}", f"hT{fc}"])
                    else:
                        tr.op("act", lambda e: e.activation(out=sq, in_=bk(bg)[:, :ntok], func=AF.Silu),
                              writes=[f"ps{bg}", f"sg{q}"])
                        tr.op("dve", lambda e: e.tensor_tensor(out=hT[:, fc, :ntok], in0=sq, in1=bk(bu)[:, :ntok], op=ALU.mult),
                              reads=[f"sg{q}"], writes=[f"ps{bu}", f"hT{fc}"])
                        if fc % 2 == 1:
                            drain_casts(1, dep=f"hT{fc}")
                    state["gu_half"] = 0
                    if inter:
                        yield
                release(ig, iu)

        def ffn_down(wn_d, blks, post=None, between=None):
            nslab = 6
            if between is not None:
                between()
            npass = (len(blks) + 1) // 2
            for ps_ in range(npass):
                pblks = blks[ps_ * 2:ps_ * 2 + 2]
                bsets = [(0, 1), (6, 7)] if ps_ % 2 == 0 else [(2, 3), (4, 5)]
                for j in range(nslab):
                    cw = 512 if j < 5 else 256
                    s = need("d", wn_d, j * 512, cw)
                    wdv = dview(s)
                    for bi, (blk, tcol) in enumerate(pblks):
                        for k in range(cw // 128):
                            fc = j * 4 + k
                            for half in range(2):
                                b = bsets[bi][half]
                                tr.op("pe", lambda e, b=b, half=half, fc=fc, k=k, tcol=tcol, wdv=wdv: e.matmul(
                                    bk(b), lhsT=hT[:, fc, tcol:tcol + 128], rhs=wdv[:, k, half * 512:(half + 1) * 512],
                                    start=(fc == 0), stop=(fc == NFC - 1)),
                                    reads=[f"wb{s}", f"hT{fc}"], writes=[f"ps{b}"], signal=(fc == NFC - 1 or k == cw // 128 - 1))
                    release(s)
                    drain_casts(1)
                for bi, (blk, tcol) in enumerate(pblks):
                    for half in range(2):
                        b = bsets[bi][half]
                        dst = xf[:, blk, half * 512:(half + 1) * 512]
                        tr.op("dve", lambda e, b=b, dst=dst: e.scalar_tensor_tensor(out=dst, in0=bk(b), scalar=0.5, in1=dst,
                                                                                    op0=ALU.mult, op1=ALU.add),
                              writes=[f"ps{b}", f"xf{blk}"])
                    if post is not None and ps_ < npass - 1:
                        post(blk)

        def rot_tables(t, defer_sin=False):
            twopi = 2.0 * math.pi
            c1 = float(np.float32(6.28125))
            c2 = float(np.float32(twopi - c1))
            c3 = float(twopi - c1 - c2)
            V = "dve"
            PJ3 = ["pj0", "pj1", "pj2"]
            tr.op(V, lambda e: e.tensor_tensor(out=ang, in0=posf[:, 4 * t:4 * t + 4].unsqueeze(2).to_broadcast([128, 4, 64]),
                                               in1=inv[:].unsqueeze(1).to_broadcast([128, 4, 64]), op=ALU.mult), writes=PJ3)
            tr.op(V, lambda e: e.tensor_scalar(out=angi, in0=ang, scalar1=float(1.0 / twopi), scalar2=None, op0=ALU.mult),
                  writes=PJ3)
            tr.op(V, lambda e: e.tensor_copy(out=angk, in_=angi), writes=PJ3)
            tr.op(V, lambda e: e.scalar_tensor_tensor(out=rsn, in0=angk, scalar=-c1, in1=ang, op0=ALU.mult, op1=ALU.add),
                  writes=PJ3)
            tr.op(V, lambda e: e.scalar_tensor_tensor(out=rsn, in0=angk, scalar=-c2, in1=rsn, op0=ALU.mult, op1=ALU.add),
                  writes=PJ3)
            tr.op(V, lambda e: e.scalar_tensor_tensor(out=rsn, in0=angk, scalar=-c3, in1=rsn, op0=ALU.mult, op1=ALU.add),
                  writes=PJ3)

            def wrap(dst, dres, src, sres, shift):
                tr.op(V, lambda e: e.tensor_scalar(out=dst, in0=src, scalar1=float(shift), scalar2=None, op0=ALU.add),
                      writes=PJ3)
                tr.op(V, lambda e: e.tensor_scalar(out=angm, in0=dst, scalar1=float(-math.pi), scalar2=float(twopi),
                                                   op0=ALU.is_lt, op1=ALU.mult), writes=PJ3)
                tr.op(V, lambda e: e.tensor_tensor(out=dst, in0=dst, in1=angm, op=ALU.add), writes=PJ3)
                tr.op(V, lambda e: e.tensor_scalar(out=angm, in0=dst, scalar1=float(math.pi), scalar2=float(-twopi),
                                                   op0=ALU.is_gt, op1=ALU.mult), writes=PJ3)
                tr.op(V, lambda e: e.tensor_tensor(out=dst, in0=dst, in1=angm, op=ALU.add), writes=PJ3)
                tr.op(V, lambda e: e.tensor_scalar(out=dst, in0=dst, scalar1=float(math.pi), scalar2=float(-math.pi),
                                                   op0=ALU.min, op1=ALU.max), writes=PJ3)
            wrap(rsn, "rsn", rsn, "rsn", 0.0)
            wrap(rcs, "rcs", rsn, "rsn", math.pi / 2)
            if not defer_sin:
                rot_sin()

        def rot_sin():
            tr.op("act", lambda e: e.activation(out=sn[:], in_=rsn, func=AF.Sin), writes=["pj2", "sn"])
            tr.op("act", lambda e: e.activation(out=cs[:], in_=rcs, func=AF.Sin), writes=["pj2", "cs"])

        def rotate(src, sres, dst, dres, blk, kind):
            if kind == "ret":
                sv = src.rearrange("p (h i two) -> p h i two", h=8, i=32, two=2)
                dv = dst.rearrange("p (h i two) -> p h i two", h=8, i=32, two=2)
                a, b_ = sv[:, :, :, 0], sv[:, :, :, 1]
                oa, ob_ = dv[:, :, :, 0], dv[:, :, :, 1]
                c = cs[:, blk, 0:32].unsqueeze(1).to_broadcast([128, 8, 32])
                s_ = sn[:, blk, 0:32].unsqueeze(1).to_broadcast([128, 8, 32])
            else:
                sv = src.rearrange("p (h two i) -> p h two i", h=8, two=2, i=32)
                dv = dst.rearrange("p (h two i) -> p h two i", h=8, two=2, i=32)
                a, b_ = sv[:, :, 0, :], sv[:, :, 1, :]
                oa, ob_ = dv[:, :, 0, :], dv[:, :, 1, :]
                c = cs[:, blk, 32:64].unsqueeze(1).to_broadcast([128, 8, 32])
                s_ = sn[:, blk, 32:64].unsqueeze(1).to_broadcast([128, 8, 32])
            t = [r_[:].rearrange("p (h i) -> p h i", h=8) for r_ in rt]
            tr.op("dve", lambda e: e.tensor_tensor(out=t[0], in0=a, in1=c, op=ALU.mult), reads=[sres, "cs"], writes=["rt0"])
            tr.op("dve", lambda e: e.tensor_tensor(out=t[1], in0=b_, in1=s_, op=ALU.mult), reads=[sres, "sn"], writes=["rt1"])
            tr.op("dve", lambda e: e.tensor_tensor(out=t[2], in0=b_, in1=c, op=ALU.mult), reads=[sres, "cs"], writes=["rt2"])
            tr.op("dve", lambda e: e.tensor_tensor(out=t[3], in0=a, in1=s_, op=ALU.mult), reads=[sres, "sn"], writes=["rt3"])
            tr.op("dve", lambda e: e.tensor_tensor(out=oa, in0=t[0], in1=t[1], op=ALU.subtract), reads=["rt0", "rt1"], writes=[dres])
            tr.op("dve", lambda e: e.tensor_tensor(out=ob_, in0=t[2], in1=t[3], op=ALU.add), reads=["rt2", "rt3"], writes=[dres])

        def win_slab(cg):
            return need("g", "win", cg * 512)

        def proj(s, tcol):
            b = 2 + state["wbank"] % 4
            state["wbank"] += 1
            for dc in range(8):
                tr.op("pe", lambda e, dc=dc: e.matmul(bk(b), lhsT=xTB[:, dc, tcol:tcol + 128], rhs=gview(s)[:, dc, :],
                                                      start=(dc == 0), stop=(dc == 7)),
                      reads=[f"wb{s}", f"xTB{tcol // 128}"], writes=[f"ps{b}"], signal=(dc == 7))
            return b

        def evac_f32(b):
            i = state["pjs"] % 3
            state["pjs"] += 1
            tr.op("act", lambda e: e.copy(out=pj[i][:], in_=bk(b)), writes=[f"ps{b}", f"pj{i}"])
            return i

        OWN_SLOTS = [(0, 1), (2, 3)]
        own_blks = [0, 2]

        def slots(t):
            o = OWN_SLOTS[t % 2]
            return [o[0], 4, o[1], 5]

        def load_x(t):
            sl4 = slots(t)
            for blk in range(4):
                L = 4 * t + blk
                tr.dma(out=xf[:, sl4[blk], :], in_=x_d[L * 128:(L + 1) * 128, :], writes=[f"xf{sl4[blk]}"], sem=f"xf{sl4[blk]}")

        def ffn(wn_g, wn_u, wn_d, ntok, blks, srcT, src_res, post=None, between=None):
            for _ in ffn_gu(wn_g, wn_u, ntok, srcT, src_res):
                pass
            ffn_down(wn_d, blks, post=post, between=between)

        def prep_A_blk(t, blk, light_act=False):
            sl4 = slots(t)
            stream_to_T(sl4[blk], xTA, "xTA", blk * 128, True, light_act=light_act)

        def prep_A(t, light_act=False):
            rot_tables(t)
            for blk in range(4):
                prep_A_blk(t, blk, light_act=light_act)

        def ffn_A_gu(t, inter=False):
            return ffn_gu("wg1", "wu1", 512, xTA, [f"xTA{i}" for i in range(4)], inter=inter)

        def ffn_A_down(t, between=None):
            sl4 = slots(t)
            ffn_down("wd1", [(sl4[b_], b_ * 128) for b_ in range(4)], post=lambda slot: layer_norm(slot, 0, 1), between=between)
            layer_norm(sl4[2], 0, 1)
            layer_norm(sl4[3], 0, 1)

        def x1_to_T(t):
            sl4 = slots(t)
            for blk in range(4):
                stream_to_T(sl4[blk], xTB, "xTB", blk * 128, blk in own_blks)

        def x2_to_T(t):
            sl4 = slots(t)
            for qi, blk in enumerate(own_blks):
                stream_to_T(sl4[blk], xTB, "xTB", qi * 128, True)

        def stage_BE(t, before_C=None, before_D=None, inter_gen=None):
            sl4 = slots(t)
            if t + 1 < nt:
                load_x(t + 1)
            items = []
            for blk in range(4):
                items.append(("rk", 1, blk, None))
            for blk in range(4):
                items.append(("rv", 2, blk, None))
            for blk in range(4):
                items.append(("dk", 5, blk, None))
            for blk in range(4):
                items.append(("dv", 6, blk, None))
            for qi, blk in enumerate(own_blks):
                items.append(("rq", 0, blk, qi))
            for qi, blk in enumerate(own_blks):
                items.append(("rg", 3, blk, qi))
            for qi, blk in enumerate(own_blks):
                items.append(("dq", 4, blk, qi))
            cg_order = [1, 2, 5, 6, 0, 3, 4]
            slab_of = {}

            prev_cg = [None]

            def issue_slab(cg):
                slab_of[cg] = win_slab(cg)
                prev_cg[0] = cg

            st_ = {}

            def P(it):
                kind, cg, blk, qi = it
                if cg not in slab_of:
                    if slab_of:
                        release(slab_of[prev_cg[0]])
                    issue_slab(cg)
                nxt = cg_order.index(cg) + 1
                if nxt < len(cg_order) and cg_order[nxt] not in slab_of and blk == (3 if qi is None else own_blks[-1]):
                    pass
                st_[it] = dict(bank=proj(slab_of[cg], blk * 128))

            def V(it):
                kind, cg, blk, qi = it
                b = st_[it]["bank"]
                if kind == "rv":
                    tr.op("act", lambda e: e.copy(out=rvb[:, blk, :], in_=bk(b)), writes=[f"ps{b}", f"rvb{blk}"])
                elif kind == "dv":
                    tr.op("act", lambda e: e.copy(out=Vaug[:, blk, :, 0:128], in_=bk(b).rearrange("p (h c) -> p h c", h=4)),
                          writes=[f"ps{b}", f"Vaug{blk}"])
                elif kind == "rg":
                    tr.op("act", lambda e: e.activation(out=sgr[:, qi, :], in_=bk(b), func=AF.Silu), writes=[f"ps{b}", f"sgr{qi}"])
                else:
                    i = evac_f32(b)
                    r = state["rot"] % 2
                    state["rot"] += 1
                    st_[it]["rot"] = r
                    rotate(pj[i][:], f"pj{i}", rot[r][:], f"rot{r}", blk, "ret" if kind in ("rk", "rq") else "rope")
                    if kind == "rk":
                        which = 0 if blk in own_blks else 1
                        tr.op("dve", lambda e: e.tensor_tensor(
                            out=Kdec[:, blk, :].rearrange("p (h d) -> p h d", h=8), in0=rot[r][:].rearrange("p (h d) -> p h d", h=8),
                            in1=kdec[:, which, :].unsqueeze(2).to_broadcast([128, 8, 64]), op=ALU.mult),
                            reads=[f"rot{r}"], writes=[f"Kdec{blk}"])

            def T(it):
                kind, cg, blk, qi = it
                if kind in ("rv", "dv", "rg"):
                    return
                r = st_[it]["rot"]
                src = lambda j: rot[r][:, j * 128:(j + 1) * 128]
                if kind == "rk":
                    transpose_blocks(src, 4, rkT[:, :, blk * 128:(blk + 1) * 128], [f"rot{r}"], [f"rkT{blk}"], evac="act")
                elif kind == "dk":
                    transpose_blocks(src, 4, dkT[:, :, blk * 128:(blk + 1) * 128], [f"rot{r}"], [f"dkT{blk}"], evac="act")
                elif kind == "rq":
                    transpose_blocks(src, 4, rqT[:, :, qi * 128:(qi + 1) * 128], [f"rot{r}"], [f"rqT{qi}"], evac="act")
                    tr.op("dve", lambda e: e.tensor_tensor(out=rqTd[:, :, qi * 128:(qi + 1) * 128], in0=rqT[:, :, qi * 128:(qi + 1) * 128],
                                                            in1=qdec[:], op=ALU.mult), reads=[f"rqT{qi}"], writes=[f"rqTd{qi}"])
                elif kind == "dq":
                    transpose_blocks(src, 4, dqT[:, :, qi * 128:(qi + 1) * 128], [f"rot{r}"], [f"dqT{qi}"], evac="act")

            def pump(n):
                if inter_gen is not None:
                    for _ in range(n):
                        next(inter_gen, None)

            state["gu_mode"] = "B"
            n_it = len(items)
            for idx in range(n_it + 2):
                if idx < n_it:
                    cg = items[idx][1]
                    P(items[idx])
                    if inter_gen is not None:
                        if 4 <= idx < 8:
                            prep_A_blk(t + 1, idx - 4)
                if 0 <= idx - 1 < n_it:
                    V(items[idx - 1])
                if 0 <= idx - 2 < n_it:
                    T(items[idx - 2])
            release(slab_of[prev_cg[0]])
            if before_C is not None:
                before_C()
            for h in range(4):
                tr.dma(out=KTd[h, :, t * 512:(t + 1) * 512], in_=dkT[:, h, :], reads=[f"dkT{i}" for i in range(4)],
                       writes=[f"kvd{t}"], sem=f"kvw{t % 2}")
                tr.dma(out=Vd[h, :, 4 * t:4 * t + 4, :], in_=Vaug[:, :, h, :], reads=[f"Vaug{i}" for i in range(4)],
                       writes=[f"kvd{t}"], sem=f"kvw{t % 2}")
            if state["gu_half"] == 1:
                pump(1)
            state["gu_mode"] = "C"
            if before_D is not None:
                rot_tables(t + 1, defer_sin=True)
            for qi in range(2):
                ob_, xb_ = 2 * qi, 2 * qi + 1
                for (kb, bA, bB, Dt, At, ares) in [(ob_, 2, 3, DoT, AoT, "AoT"), (xb_, 4, 5, DxT, AxT, "AxT")]:
                    for h in range(8):
                        i_, hh = h // 2, h % 2
                        b = bA if hh == 0 else bB
                        tr.op("pe", lambda e, b=b, h=h, i_=i_, hh=hh, kb=kb: e.matmul(
                            bk(b)[:, i_ * 128:(i_ + 1) * 128],
                            lhsT=rkT[hh * 64:(hh + 1) * 64, i_, kb * 128:(kb + 1) * 128],
                            rhs=rqT[hh * 64:(hh + 1) * 64, i_, qi * 128:(qi + 1) * 128], start=True, stop=True),
                            reads=[f"rkT{kb}", f"rqT{qi}"], writes=[f"ps{b}"], signal=(h >= 6))
                    for half, b in enumerate([bA, bB]):
                        tr.op("dve", lambda e, b=b, half=half, Dt=Dt, At=At: e.tensor_tensor(
                            out=At[:, half * 4:(half + 1) * 4, :], in0=bk(b).rearrange("p (h q) -> p h q", h=4),
                            in1=Dt[:, half * 4:(half + 1) * 4, :], op=ALU.mult), writes=[f"ps{b}", ares + str(half)])
                first = True
                for h in range(8):
                    i_, hh = h // 2, h % 2
                    half = hh
                    sl_ = hh * 4 + i_
                    oc = bk(6)[:, h * 64:(h + 1) * 64]
                    tr.op("pe", lambda e, oc=oc, h=h, first=first, sl_=sl_: e.matmul(oc, lhsT=AoT[:, sl_, :], rhs=rvb[:, ob_, h * 64:(h + 1) * 64],
                                                                            start=first, stop=False, skip_group_check=True),
                          reads=[f"AoT{half}", f"rvb{ob_}"], writes=["ps6"], signal=False)
                    first = False
                    tr.op("pe", lambda e, oc=oc, h=h, sl_=sl_: e.matmul(oc, lhsT=AxT[:, sl_, :], rhs=rvb[:, xb_, h * 64:(h + 1) * 64],
                                                               start=False, stop=False, skip_group_check=True),
                          reads=[f"AxT{half}", f"rvb{xb_}"], writes=["ps6"], signal=False)
                    tr.op("pe", lambda e, oc=oc, h=h, i_=i_, hh=hh: e.matmul(
                        oc, lhsT=rqTd[hh * 64:(hh + 1) * 64, i_, qi * 128:(qi + 1) * 128], rhs=Sbf[hh * 64:(hh + 1) * 64, i_, :],
                        start=False, stop=(h == 7), skip_group_check=True),
                        reads=[f"rqTd{qi}", "Sbf"], writes=["ps6"], signal=(h == 7))
                pump(3)
                for i_ in range(4):
                    for n_, kb in enumerate([ob_, xb_]):
                        tr.op("pe", lambda e, i_=i_, kb=kb, n_=n_: e.matmul(
                            bk(7)[:, i_ * 128:(i_ + 1) * 128], lhsT=Kdec[:, kb, i_ * 128:(i_ + 1) * 128],
                            rhs=rvb[:, kb, i_ * 128:(i_ + 1) * 128], start=(i_ == 0 and n_ == 0), stop=(i_ == 3 and n_ == 1),
                            skip_group_check=True),
                            reads=[f"Kdec{kb}", f"rvb{kb}"], writes=["ps7"], signal=(i_ == 3 and n_ == 1))
                tr.op("dve", lambda e: e.tensor_tensor(out=Sst[:], in0=Sst[:], in1=Gt[:], op=ALU.mult), reads=["Sbf"], writes=["Sst"])
                for hh in range(2):
                    tr.op("dve", lambda e, hh=hh: e.tensor_tensor(
                        out=Sst[hh * 64:(hh + 1) * 64, :, :], in0=Sst[hh * 64:(hh + 1) * 64, :, :],
                        in1=bk(7)[hh * 64:(hh + 1) * 64, :].rearrange("p (i c) -> p i c", i=4)[:, :, hh * 64:(hh + 1) * 64],
                        op=ALU.add), writes=["ps7", "Sst"])
                tr.op("dve", lambda e: e.tensor_copy(out=Sbf[:], in_=Sst[:]), reads=["Sst"], writes=["Sbf"])
                pump(3)
                tr.op("act", lambda e: e.copy(out=yr, in_=bk(6).rearrange("p (h c) -> p h c", h=8)), writes=["ps6", "pj1"])
                tr.op("dve", lambda e: e.tensor_reduce(out=hst[:, 0, :], in_=yr, axis=AX.X, op=ALU.add), reads=["pj1"], writes=["hst0"])
                tr.op("dve", lambda e: e.tensor_tensor(out=ysq, in0=yr, in1=yr, op=ALU.mult), reads=["pj1"], writes=["pj0"])
                tr.op("dve", lambda e: e.tensor_reduce(out=hst[:, 1, :], in_=ysq, axis=AX.X, op=ALU.add), reads=["pj0"], writes=["hst1"])
                tr.op("dve", lambda e: e.tensor_scalar(out=hst[:, 2, :], in0=hst[:, 0, :], scalar1=1.0 / 64.0, scalar2=None, op0=ALU.mult),
                      reads=["hst0"], writes=["hst2"])
                tr.op("dve", lambda e: e.tensor_tensor(out=hst[:, 0, :], in0=hst[:, 2, :], in1=hst[:, 2, :], op=ALU.mult),
                      reads=["hst2"], writes=["hst0"])
                tr.op("dve", lambda e: e.scalar_tensor_tensor(out=hst[:, 3, :], in0=hst[:, 1, :], scalar=1.0 / 64.0, in1=hst[:, 0, :],
                                                              op0=ALU.mult, op1=ALU.subtract), reads=["hst0", "hst1"], writes=["hst3"])
                tr.op("dve", lambda e: e.tensor_scalar(out=hst[:, 3, :], in0=hst[:, 3, :], scalar1=float(NORM_EPS), scalar2=None, op0=ALU.add),
                      writes=["hst3"])
                tr.op("pool", lambda e: e.tensor_tensor(out=hst[:, 3, :], in0=hst[:, 3, :], in1=neghalf[:], op=ALU.pow), writes=["hst3"])
                tr.op("dve", lambda e: e.tensor_tensor(out=yr, in0=yr, in1=hst[:, 2, :].unsqueeze(2).to_broadcast([128, 8, 64]),
                                                       op=ALU.subtract), reads=["hst2"], writes=["pj1"])
                tr.op("dve", lambda e: e.tensor_tensor(out=yr, in0=yr, in1=hst[:, 3, :].unsqueeze(2).to_broadcast([128, 8, 64]),
                                                       op=ALU.mult), reads=["hst3"], writes=["pj1"])
                tr.op("dve", lambda e: e.tensor_tensor(out=yr, in0=yr, in1=nw[:, 0, :].rearrange("p (h c) -> p h c", h=8),
                                                        op=ALU.mult), writes=["pj1"])
                tr.op("dve", lambda e, qi=qi: e.tensor_tensor(out=merged[:, qi, 0:512], in0=yr.rearrange("p h c -> p (h c)"),
                                                               in1=sgr[:, qi, :], op=ALU.mult), reads=["pj1", f"sgr{qi}"], writes=[f"mgr{qi}"])

            pump(2)
            if state["gu_half"] == 1:
                pump(1)
            state["gu_mode"] = "D"
            if before_D is not None:
                rot_sin()
            obank = {(0, 0): (4, 0), (0, 1): (4, 256), (1, 0): (5, 0), (1, 1): (5, 256)}
            gen_live = [inter_gen is not None]
            for h in range(4):
                started = set()
                steps = []
                for c in range(t):
                    steps.append(("hist", c, (0, 1)))
                    steps.append(("hist", c, (2, 3)))
                steps.append(("cur", t, (0, 1)))
                steps.append(("cur", t, (2, 3)))
                cur_k = (dkT[:, h, :], [f"dkT{i}" for i in range(4)])
                cur_v = (Vaug[:, :, h, :], [f"Vaug{i}" for i in range(4)])
                slot_of = {}
                info = {}

                def do_qk(si):
                    kind, c, kbs = steps[si]
                    if kind == "hist":
                        if c not in slot_of:
                            sl = state["kv"] % 2
                            state["kv"] += 1
                            slot_of[c] = sl
                            tr.dma(out=KTp[sl][:], in_=KTd[h, :, c * 512:(c + 1) * 512], reads=[f"kvd{c}"], writes=[f"KTp{sl}"], sem=f"KTp{sl}")
                            tr.dma(out=Vp[sl][:], in_=Vd[h, :, 4 * c:4 * c + 4, :], reads=[f"kvd{c}"], writes=[f"Vp{sl}"], sem=f"Vp{sl}")
                        sl = slot_of[c]
                        kt_ap, kres = KTp[sl], [f"KTp{sl}"]
                        vt = (Vp[sl], [f"Vp{sl}"])
                    else:
                        kt_ap, kres = cur_k
                        vt = cur_v
                    if gen_live[0]:
                        b = (2, 3)
                    else:
                        b = (0, 1) if state["lastS"] == (2, 3) else (2, 3)
                    state["lastS"] = b
                    only_b = (kind == "cur" and kbs == (2, 3))
                    for j, kb in enumerate(kbs):
                        for sub in range(2):
                            if only_b:
                                o_ = bk(b[sub])[:, j * 256 + 128:(j + 1) * 256]
                                r_ = dqT[sub * 64:(sub + 1) * 64, h, 128:256]
                                rres = ["dqT1"]
                            else:
                                o_ = bk(b[sub])[:, j * 256:(j + 1) * 256]
                                r_ = dqT[sub * 64:(sub + 1) * 64, h, :]
                                rres = ["dqT0", "dqT1"]
                            tr.op("pe", lambda e, o_=o_, r_=r_, sub=sub, kb=kb, kt_ap=kt_ap: e.matmul(
                                o_, lhsT=kt_ap[sub * 64:(sub + 1) * 64, kb * 128:(kb + 1) * 128], rhs=r_, start=True, stop=True),
                                reads=kres + rres, writes=[f"ps{b[sub]}"], signal=(j == 1))
                    info[si] = (b, vt)

                def ex(b, p, sub, c0, c1, special):
                    kw = dict(bias=sbias[:, 0:1]) if special else {}
                    tr.op("act", lambda e: e.activation(out=PT[p][:, sub * 512 + c0:sub * 512 + c1], in_=bk(b[sub])[:, c0:c1],
                                                        func=AF.Exp, scale=0.125, **kw),
                          writes=[f"ps{b[sub]}", f"PT{p}"])

                def do_exp(si):
                    kind, c, kbs = steps[si]
                    b, (v_ap, vres) = info[si]
                    p = state["pt"] % 2
                    state["pt"] += 1
                    ptv = PT[p][:].rearrange("p (s q) -> p s q", s=2)
                    if kind == "hist":
                        for sub in range(2):
                            ex(b, p, sub, 0, 512, False)
                        plan = [(0, kbs[0], [0, 1], []), (1, kbs[1], [0, 1], [])]
                    elif kbs == (0, 1):
                        for sub in range(2):
                            ex(b, p, sub, 0, 256, False)
                            ex(b, p, sub, 256, 384, True)
                            ex(b, p, sub, 384, 512, False)
                        tr.op("dve", lambda e: e.tensor_tensor(out=ptv[:, :, 0:128], in0=ptv[:, :, 0:128],
                                                               in1=tri[:].unsqueeze(1).to_broadcast([128, 2, 128]), op=ALU.mult),
                              writes=[f"PT{p}"])
                        plan = [(0, 0, [0, 1], []), (1, 1, [0, 1], [0])]
                    else:
                        for sub in range(2):
                            ex(b, p, sub, 128, 256, False)
                            ex(b, p, sub, 384, 512, True)
                        tr.op("dve", lambda e: e.tensor_tensor(out=ptv[:, :, 128:256], in0=ptv[:, :, 128:256],
                                                               in1=tri[:].unsqueeze(1).to_broadcast([128, 2, 128]), op=ALU.mult),
                              writes=[f"PT{p}"])
                        plan = [(0, 2, [1], []), (1, 3, [1], [1])]
                    info[si] = (b, (v_ap, vres), p, plan)

                def do_pv(si):
                    b, (v_ap, vres), p, plan = info[si]
                    for (j, kb, qis, last_for) in plan:
                        for sub in range(2):
                            for qi in qis:
                                ob__, oc0 = obank[(sub, qi)]
                                st = ob__ not in started
                                started.add(ob__)
                                c0 = sub * 512 + j * 256 + qi * 128
                                tr.op("pe", lambda e, ob__=ob__, oc0=oc0, c0=c0, st=st, kb=kb: e.matmul(
                                    bk(ob__)[:, oc0:oc0 + 129], lhsT=PT[p][:, c0:c0 + 128], rhs=v_ap[:, kb, 0:129],
                                    start=st, stop=False, skip_group_check=True),
                                    reads=[f"PT{p}"] + vres, writes=[f"ps{ob__}"], signal=(sub == 1 and qi == qis[-1]))

                qk_done = set()
                for si in range(len(steps)):
                    if si not in qk_done:
                        do_qk(si)
                        qk_done.add(si)
                    if gen_live[0]:
                        do_exp(si)
                        if next(inter_gen, "done") == "done":
                            gen_live[0] = False
                        do_pv(si)
                    else:
                        if si + 1 < len(steps):
                            do_qk(si + 1)
                            qk_done.add(si + 1)
                        do_exp(si)
                        do_pv(si)
                for qi in range(2):
                    for sub in range(2):
                        ob__, oc0 = obank[(sub, qi)]
                        tr.op("act", lambda e, ob__=ob__, oc0=oc0, sub=sub: e.copy(out=ob[:, sub, 0:129], in_=bk(ob__)[:, oc0:oc0 + 129]),
                              writes=[f"ps{ob__}", "ob"])
                    tr.op("dve", lambda e: e.reciprocal(out=dst_[:, 0:2], in_=ob[:, :, 128]), reads=["ob"], writes=["dst"])
                    tr.op("dve", lambda e: e.tensor_tensor(out=dst_[:, 2:3], in0=dst_[:, 1:2], in1=neglam[:], op=ALU.mult), writes=["dst"])
                    tr.op("dve", lambda e: e.tensor_scalar(out=dfa, in0=ob[:, 0, 0:128], scalar1=dst_[:, 0:1], scalar2=None, op0=ALU.mult),
                          reads=["ob", "dst"], writes=["pj2"])
                    tr.op("dve", lambda e: e.scalar_tensor_tensor(out=dfa, in0=ob[:, 1, 0:128], scalar=dst_[:, 2:3], in1=dfa,
                                                                  op0=ALU.mult, op1=ALU.add), reads=["ob", "dst"], writes=["pj2"])
                    tr.op("dve", lambda e: e.tensor_tensor(out=dfb, in0=dfa, in1=dfa, op=ALU.mult), reads=["pj2"], writes=["pj2"])
                    tr.op("dve", lambda e: e.tensor_reduce(out=dst_[:, 3:4], in_=dfb, axis=AX.X, op=ALU.add), reads=["pj2"], writes=["dst"])
                    tr.op("dve", lambda e: e.tensor_scalar(out=dst_[:, 3:4], in0=dst_[:, 3:4], scalar1=1.0 / 128.0, scalar2=float(NORM_EPS),
                                                           op0=ALU.mult, op1=ALU.add), writes=["dst"])
                    tr.op("pool", lambda e: e.tensor_tensor(out=dst_[:, 3:4], in0=dst_[:, 3:4], in1=neghalf[:, 0:1], op=ALU.pow), writes=["dst"])
                    tr.op("dve", lambda e: e.tensor_scalar(out=dfa, in0=dfa, scalar1=dst_[:, 3:4], scalar2=float(1.0 - LAMBDA_INIT),
                                                           op0=ALU.mult, op1=ALU.mult), reads=["dst"], writes=["pj2"])
                    tr.op("dve", lambda e, qi=qi, h=h: e.tensor_tensor(out=merged[:, qi, 512 + h * 128:512 + (h + 1) * 128], in0=dfa,
                                                                        in1=nw[:, 1, h * 128:(h + 1) * 128], op=ALU.mult),
                          reads=["pj2"], writes=[f"mgd{qi}_{h}"])

            if inter_gen is not None:
                for _ in inter_gen:
                    pass
            for qi in range(2):
                transpose_blocks(lambda j, qi=qi: merged[:, qi, j * 128:(j + 1) * 128], 8, xTB[:, :, qi * 128:(qi + 1) * 128],
                                 [f"mgr{qi}"] + [f"mgd{qi}_{h}" for h in range(4)], [f"xTB{qi}"], evac="act")
            for half in range(2):
                s = need("g", "wout", half * 512)
                for qi, blk in enumerate(own_blks):
                    b = 4 + half * 2 + qi
                    for mc in range(8):
                        tr.op("pe", lambda e, b=b, mc=mc, qi=qi: e.matmul(bk(b), lhsT=xTB[:, mc, qi * 128:(qi + 1) * 128], rhs=gview(s)[:, mc, :],
                                                                          start=(mc == 0), stop=(mc == 7)),
                              reads=[f"wb{s}", f"xTB{qi}"], writes=[f"ps{b}"], signal=(mc == 7))
                    dst = xf[:, sl4[blk], half * 512:(half + 1) * 512]
                    tr.op("dve", lambda e, b=b, dst=dst: e.tensor_tensor(out=dst, in0=bk(b), in1=dst, op=ALU.add),
                          writes=[f"ps{b}", f"xf{sl4[blk]}"])
                release(s)
            for blk in own_blks:
                layer_norm(sl4[blk], 2, 3)

        def stage_F(t, between=None):
            sl4 = slots(t)
            ffn("wg2", "wu2", "wd2", 256, [(sl4[blk], qi * 128) for qi, blk in enumerate(own_blks)], xTB, ["xTB0", "xTB1"],
                between=between)
            for qi, blk in enumerate(own_blks):
                layer_norm(sl4[blk], 4, 5)
                g = 2 * t + qi
                tr.dma(out=out_d[g * 128:(g + 1) * 128, :], in_=xf[:, sl4[blk], :], reads=[f"xf{sl4[blk]}"], writes=[],
                       sem=f"out{sl4[blk]}")

        load_x(0)
        prep_A(0)
        for _ in ffn_A_gu(0):
            pass
        ffn_A_down(0)
        drain_casts(len(pending_casts))
        x1_to_T(0)
        for t in range(nt):
            last = (t + 1 == nt)
            if last:
                stage_BE(t)
                x2_to_T(t)
                stage_F(t)
            else:
                stage_BE(t, before_D=lambda t=t: rot_tables(t + 1), inter_gen=ffn_A_gu(t + 1, inter=True))
                ffn_A_down(t + 1, between=lambda t=t: x2_to_T(t))
                stage_F(t, between=lambda t=t: x1_to_T(t + 1))

        for key in [k for k in tr.semh if k.startswith("dma:out")]:
            nc.sync.wait_ge(tr.semh[key], tr.cnt[key])
        if not recording:
            assert ring_state["consumed"] == len(plan)
            print("instructions:", tr.n_inst)
    if recording:
        return build_program(nt=nt, stop=stop, plan_in=plan)
    return nc


_NC_CACHE = {}


def _tables(p):
    h = np.arange(8, dtype=np.float64)
    gam = 1.0 - 2.0 ** (-5.0 - h)
    k = np.arange(128)[:, None]
    q = np.arange(128)[None, :]
    rel = (q - k).astype(np.float64)
    DoT = np.zeros((128, 8, 128), np.float64)
    DxT = np.zeros((128, 8, 128), np.float64)
    for hd in range(8):
        sl = (hd % 2) * 4 + hd // 2
        DoT[:, sl, :] = np.where(rel >= 0, 0.125 * gam[hd] ** np.maximum(rel, 0.0), 0.0)
        if p == 1:
            DxT[:, sl, :] = 0.125 * gam[hd] ** (128.0 + rel)
    qdec = np.zeros((128, 4, 128), np.float64)
    Gt = np.zeros((128, 4, 64), np.float64)
    for i in range(4):
        for hh in range(2):
            g_ = gam[2 * i + hh]
            qdec[hh * 64:(hh + 1) * 64, i, :] = (g_ ** (np.arange(128) + 1.0 + 128.0 * p))[None, :]
            Gt[hh * 64:(hh + 1) * 64, i, :] = g_ ** 256.0
    kdec = np.zeros((128, 2, 8), np.float64)
    tt = np.arange(128, dtype=np.float64)
    for hh in range(8):
        kdec[:, 0, hh] = 0.125 * gam[hh] ** (255.0 - (tt + 128.0 * p))
        kdec[:, 1, hh] = 0.125 * gam[hh] ** (255.0 - (tt + 128.0 * (1 - p)))
    tri = (np.arange(128)[:, None] <= np.arange(128)[None, :]).astype(np.float32)
    inv_ret = 1.0 / (10000.0 ** np.linspace(0.0, 1.0, 32, dtype=np.float32))
    inv_rope = 1.0 / (10000.0 ** (np.arange(0, 64, 2, dtype=np.float32) / 64.0))
    inv = np.concatenate([inv_ret.astype(np.float32), inv_rope.astype(np.float32)])[None, :].repeat(128, 0)
    sb = np.full((128, 1), 0.0 if p == 1 else NEG_BIG, np.float32)
    f = lambda a: np.ascontiguousarray(a.astype(np.float32))
    return dict(DoT=f(DoT), DxT=f(DxT), qdec=f(qdec), Gt=f(Gt), kdec=f(kdec), tri=f(tri),
                ident=np.eye(128, dtype=np.float32), inv=f(inv), sbias=sb)


def kernel(x, positions, ffn1_w_gate, ffn1_w_up, ffn1_w_down, ln1_w, ln1_b,
           w_in, ret_norm_w, diff_lambda_q1, diff_lambda_k1, diff_lambda_q2, diff_lambda_k2,
           diff_norm_w, w_out, ln2_w, ln2_b, ffn2_w_gate, ffn2_w_up, ffn2_w_down, ln3_w, ln3_b):
    if "nc" not in _NC_CACHE:
        _NC_CACHE["nc"] = build_program()
    nc = _NC_CACHE["nc"]
    A = lambda a: np.ascontiguousarray(np.asarray(a))
    x = A(x).astype(np.float32, copy=False)
    positions = A(positions).astype(np.int32, copy=False)
    rep = lambda v: np.ascontiguousarray(np.broadcast_to(np.asarray(v, np.float32).reshape(1, -1), (128, np.asarray(v).size)))
    lnv = np.ascontiguousarray(np.stack([rep(ln1_w), rep(ln1_b), rep(ln2_w), rep(ln2_b), rep(ln3_w), rep(ln3_b)], axis=1))
    nwv = np.ascontiguousarray(np.stack([rep(ret_norm_w), rep(diff_norm_w)], axis=1))
    lamv = np.ascontiguousarray(np.stack([rep(diff_lambda_q1), rep(diff_lambda_k1), rep(diff_lambda_q2), rep(diff_lambda_k2)], axis=1))
    shared = {
        "wg1": A(ffn1_w_gate)[0], "wu1": A(ffn1_w_up)[0], "wd1": A(ffn1_w_down)[0],
        "win": A(w_in)[0], "wout": A(w_out)[0],
        "wg2": A(ffn2_w_gate)[0], "wu2": A(ffn2_w_up)[0], "wd2": A(ffn2_w_down)[0],
        "lnv": lnv, "nw": nwv, "lam": lamv,
    }
    shared = {k: np.ascontiguousarray(v, dtype=np.float32) for k, v in shared.items()}
    tabs = [_tables(0), _tables(1)]
    in_maps = []
    for c in range(8):
        b, p = c // 2, c % 2
        order = []
        for g in range(NBLK // 2):
            order += [2 * g + p, 2 * g + 1 - p]
        xb = x[b].reshape(NBLK, 128, D)[order].reshape(SEQ, D)
        pb = positions[b].reshape(NBLK, 128)[order]
        m = dict(shared)
        m["x"] = np.ascontiguousarray(xb)
        m["pos"] = np.ascontiguousarray(pb.T)
        m.update(tabs[p])
        in_maps.append(m)
    res = run_bass_kernel_spmd(nc, in_maps, core_ids=list(range(8)))
    out = np.empty((BATCH, SEQ, D), np.float32)
    for c in range(8):
        b, p = c // 2, c % 2
        o = np.asarray(res.results[c]["out"]).reshape(NBLK // 2, 128, D)
        ov = out[b].reshape(NBLK, 128, D)
        for g in range(NBLK // 2):
            ov[2 * g + p] = o[g]
    return out
```

```python
import math
import contextlib
import numpy as np
import concourse.bass as bass
import concourse.mybir as mybir
from concourse.bass_utils import run_bass_kernel_spmd

F32 = mybir.dt.float32
BF16 = mybir.dt.bfloat16
I32 = mybir.dt.int32
AF = mybir.ActivationFunctionType
ALU = mybir.AluOpType
AX = mybir.AxisListType

D = 1024
FF = 2816
SEQ = 8192
BATCH = 4
NBLK = SEQ // 128
NT = 16
INW = 3584
ALPHA = (2.0) ** 0.25
LAMBDA_INIT = 0.8 - 0.6 * math.exp(0.0)
LN_EPS = 1e-5
NORM_EPS = 1e-6
NEG_BIG = -30000.0
NFC = FF // 128


class Tracker:
    def __init__(self, nc, es):
        self.nc = nc
        self.es = es
        self.eng = {"pe": nc.tensor, "act": nc.scalar, "dve": nc.vector, "pool": nc.gpsimd, "sp": nc.sync}
        self.semh = {}
        self.cnt = {}
        for k in ["pe", "act", "dve", "pool"]:
            self.semh[k] = es.enter_context(nc.semaphore("s_" + k))
            self.cnt[k] = 0
        self.last_w = {}
        self.readers = {}
        self.waited = {e: {} for e in self.eng}
        self.n_inst = 0

    def dsem(self, name):
        key = "dma:" + name
        if key not in self.semh:
            self.semh[key] = self.es.enter_context(self.nc.semaphore("d_" + name))
            self.cnt[key] = 0
        return key

    def _wait(self, e, dep):
        key, val = dep
        if e == "pe" and key == "pe":
            return
        if self.waited[e].get(key, 0) >= val:
            return
        if key == "pe":
            assert val <= self.cnt["pe"], "dependency on unsignaled PE op"
        self.eng[e].wait_ge(self.semh[key], val)
        self.waited[e][key] = val

    def _deps(self, e, reads, writes):
        deps = []
        for r in reads:
            if r in self.last_w:
                deps.append(self.last_w[r])
        for w in writes:
            if w in self.last_w:
                deps.append(self.last_w[w])
            deps.extend(self.readers.get(w, ()))
        for d in deps:
            self._wait(e, d)

    def _record(self, me, reads, writes):
        for r in reads:
            self.readers.setdefault(r, []).append(me)
        for w in writes:
            self.last_w[w] = me
            self.readers[w] = []

    def op(self, e, fn, reads=(), writes=(), signal=True):
        self._deps(e, reads, writes)
        inst = fn(self.eng[e])
        self.n_inst += 1
        if e == "pe" and not signal:
            me = ("pe", self.cnt["pe"] + 1)
        else:
            self.cnt[e] += 1
            inst.then_inc(self.semh[e], 1)
            me = (e, self.cnt[e])
        self._record(me, reads, writes)

    def dma(self, out, in_, reads=(), writes=(), sem=None, q="sp"):
        key = self.dsem(sem)
        self._deps(q, reads, writes)
        inst = self.eng[q].dma_start(out=out, in_=in_)
        self.cnt[key] += 16
        inst.then_inc(self.semh[key], 16)
        self.n_inst += 1
        me = (key, self.cnt[key])
        self._record(me, reads, writes)

    def barrier_all(self, res):
        dep = self.last_w[res]
        for e in self.eng:
            self._wait(e, dep)


def build_program(nt=NT, stop=None, plan_in=None):
    nc = bass.Bass("TRN2", target_bir_lowering=False)

    def din(name, shape, dt=F32):
        return nc.dram_tensor(name, list(shape), dt, kind="ExternalInput").ap()

    def dint(name, shape, dt=BF16):
        return nc.dram_tensor(name, list(shape), dt, kind="Internal").ap()

    x_d = din("x", [SEQ, D])
    pos_d = din("pos", [128, NBLK], I32)
    wsrc = {
        "wg1": din("wg1", [D, FF]), "wu1": din("wu1", [D, FF]), "wd1": din("wd1", [FF, D]),
        "win": din("win", [D, INW]), "wout": din("wout", [D, D]),
        "wg2": din("wg2", [D, FF]), "wu2": din("wu2", [D, FF]), "wd2": din("wd2", [FF, D]),
    }
    lnv_d = din("lnv", [128, 6, D])
    nw_d = din("nw", [128, 2, 512])
    lam_d = din("lam", [128, 4, 64])
    DoT_d = din("DoT", [128, 8, 128])
    DxT_d = din("DxT", [128, 8, 128])
    qdec_d = din("qdec", [128, 4, 128])
    G_d = din("Gt", [128, 4, 64])
    kdec_d = din("kdec", [128, 2, 8])
    tri_d = din("tri", [128, 128])
    ident_d = din("ident", [128, 128])
    inv_d = din("inv", [128, 64])
    sbias_d = din("sbias", [128, 1])
    out_d = nc.dram_tensor("out", [SEQ // 2, D], F32, kind="ExternalOutput").ap()

    wb = {k: dint(k + "b", v.shape) for k, v in wsrc.items()}
    KTd = dint("KTd", [4, 128, SEQ])
    Vd = dint("Vd", [4, 128, NBLK, 130])

    es = contextlib.ExitStack()
    with es:
        def sb(name, shape, dt=F32):
            return es.enter_context(nc.sbuf_tensor("sb_" + name, list(shape), dt))

        tr = Tracker(nc, es)

        xf = sb("xf", [128, 6, D])
        xbf = [sb(f"xbf{i}", [128, D], BF16) for i in range(2)]
        xTA = sb("xTA", [128, 8, 512], BF16)
        xTB = sb("xTB", [128, 8, 512], BF16)
        hT = sb("hT", [128, NFC, 512], BF16)
        sg = [sb(f"sg{i}", [128, 512]) for i in range(2)]
        wbuf = [sb(f"wbuf{i}", [128, 4096], BF16) for i in range(4)]
        lnv = sb("lnv", [128, 6, D])
        nw = sb("nw", [128, 2, 512])
        DoT = sb("DoT", [128, 8, 128])
        DxT = sb("DxT", [128, 8, 128])
        qdec = sb("qdec", [128, 4, 128])
        Gt = sb("Gt", [128, 4, 64])
        kdec = sb("kdec", [128, 2, 8])
        tri = sb("tri", [128, 128], BF16)
        ident = sb("ident", [128, 128], BF16)
        inv = sb("inv", [128, 64])
        sbias = sb("sbias", [128, 1])
        posi = sb("posi", [128, NBLK], I32)
        posf = sb("posf", [128, NBLK])
        neghalf = sb("neghalf", [128, 8])
        neglam = sb("neglam", [128, 1])
        lams = sb("lams", [128, 4])
        sn = sb("sn", [128, 4, 64])
        cs = sb("cs", [128, 4, 64])
        bst = sb("bst", [128, 2, 6])
        mv = sb("mv", [128, 2])
        rstd = sb("rstd", [128, 1])
        pj = [sb(f"pj{i}", [128, 512]) for i in range(3)]
        _v = lambda tns, a, b: tns[:, a:b].rearrange("p (x y) -> p x y", x=4)
        ang, angk = _v(pj[0], 0, 256), _v(pj[0], 256, 512)
        angi_t = sb("angi", [128, 4, 64], I32)
        angi, angm = angi_t[:], _v(pj[1], 256, 512)
        rsn, rcs = _v(pj[2], 0, 256), _v(pj[2], 256, 512)
        ysq = pj[0][:].rearrange("p (h c) -> p h c", h=8)
        yr = pj[1][:].rearrange("p (h c) -> p h c", h=8)
        dfa, dfb = pj[2][:, 0:128], pj[2][:, 128:256]
        rt = [sb(f"rt{i}", [128, 256]) for i in range(4)]
        rot = [sb(f"rot{i}", [128, 512], BF16) for i in range(2)]
        rvb = sb("rvb", [128, 4, 512], BF16)
        Vaug = sb("Vaug", [128, 4, 4, 130], BF16)
        sgr = sb("sgr", [128, 2, 512])
        Kdec = sb("Kdec", [128, 4, 512], BF16)
        rkT = sb("rkT", [128, 4, 512], BF16)
        dkT = sb("dkT", [128, 4, 512], BF16)
        rqT = sb("rqT", [128, 4, 256], BF16)
        rqTd = sb("rqTd", [128, 4, 256], BF16)
        dqT = sb("dqT", [128, 4, 256], BF16)
        AoT = sb("AoT", [128, 8, 128], BF16)
        AxT = sb("AxT", [128, 8, 128], BF16)
        Sst = sb("Sst", [128, 4, 64])
        Sbf = sb("Sbf", [128, 4, 64], BF16)
        hst = sb("hst", [128, 4, 8])
        PT = [sb(f"PT{i}", [128, 1024], BF16) for i in range(2)]
        merged = sb("merged", [128, 2, D], BF16)

        banks = [es.enter_context(nc.psum_tensor(f"bank{i}", [128, 512], F32)) for i in range(8)]
        es_init = contextlib.ExitStack()
        sbi = lambda name, shape, dt=F32: es_init.enter_context(nc.sbuf_tensor("sb_" + name, list(shape), dt))
        lam = sbi("lam", [128, 4, 64])
        lamt = sbi("lamt", [128, 4, 64])
        trif = sbi("trif", [128, 128])
        identf = sbi("identf", [128, 128])

        def bk(i):
            return banks[i][:]

        def bkb(i):
            return banks[i][:].bitcast(BF16)

        def cast_pieces(name):
            src = wsrc[name]
            dst = wb[name]
            R, C = src.shape
            k = 1
            while C // k > 2048:
                k *= 2
            sv = src.rearrange("r (k c) -> (r k) c", k=k) if k > 1 else src
            dv = dst.rearrange("r (k c) -> (r k) c", k=k) if k > 1 else dst
            rows = R * k
            step = 512
            chunks = [(r0, min(rows, r0 + step)) for r0 in range(0, rows, step)]
            out = []
            for ci, (r0, r1) in enumerate(chunks):
                def piece(dep=None, r0=r0, r1=r1, lastp=(ci == len(chunks) - 1)):
                    if dep is not None and dep in tr.last_w:
                        tr._wait("pool", tr.last_w[dep])
                    tr.dma(out=dv[r0:r1, :], in_=sv[r0:r1, :], reads=[], writes=[], sem="c_" + name, q="pool")
                    if lastp:
                        tr.last_w["w_" + name] = ("dma:c_" + name, tr.cnt["dma:c_" + name])
                out.append(piece)
            return out

        def cast_weight(name):
            for p_ in cast_pieces(name):
                p_()

        pending_casts = []

        def drain_casts(n, dep=None):
            for _ in range(n):
                if pending_casts:
                    pending_casts.pop(0)(dep)

        consts = [(lnv, lnv_d), (nw, nw_d), (lam, lam_d), (DoT, DoT_d), (DxT, DxT_d), (qdec, qdec_d),
                  (Gt, G_d), (kdec, kdec_d), (trif, tri_d), (identf, ident_d), (inv, inv_d),
                  (sbias, sbias_d), (posi, pos_d)]
        for t_, d_ in consts:
            tr.dma(out=t_[:], in_=d_, writes=["consts"], sem="consts")
        for name in ["wg1", "wu1", "wd1"]:
            cast_weight(name)
        for name in ["win", "wout", "wg2", "wu2", "wd2"]:
            pending_casts.extend(cast_pieces(name))
        tr.barrier_all("consts")
        del tr.last_w["consts"]

        tr.op("dve", lambda e: e.tensor_copy(out=tri[:], in_=trif[:]), writes=["tri"])
        tr.op("dve", lambda e: e.tensor_copy(out=ident[:], in_=identf[:]), writes=["ident"])
        tr.op("dve", lambda e: e.tensor_copy(out=posf[:], in_=posi[:]), writes=["posf"])
        tr.op("dve", lambda e: e.memset(neghalf[:], -0.5), writes=["neghalf"])
        tr.op("dve", lambda e: e.memset(Vaug[:].rearrange("p a b c -> p (a b) c")[:, :, 128:130], 1.0), writes=["Vaug"])
        tr.op("dve", lambda e: e.memset(Sst[:], 0.0), writes=["Sst"])
        tr.op("dve", lambda e: e.memset(Sbf[:], 0.0), writes=["Sbf"])
        tr.op("dve", lambda e: e.tensor_tensor(out=lamt[:, 0, :], in0=lam[:, 0, :], in1=lam[:, 1, :], op=ALU.mult), writes=["lamt"])
        tr.op("dve", lambda e: e.tensor_tensor(out=lamt[:, 1, :], in0=lam[:, 2, :], in1=lam[:, 3, :], op=ALU.mult), writes=["lamt"])
        tr.op("dve", lambda e: e.tensor_reduce(out=lams[:, 0:2], in_=lamt[:, 0:2, :], axis=AX.X, op=ALU.add), reads=["lamt"], writes=["lams"])
        tr.op("act", lambda e: e.activation(out=lams[:, 2:4], in_=lams[:, 0:2], func=AF.Exp), reads=[], writes=["lams"])
        tr.op("dve", lambda e: e.tensor_tensor(out=neglam[:], in0=lams[:, 3:4], in1=lams[:, 2:3], op=ALU.subtract), reads=["lams"], writes=["neglam"])
        tr.op("dve", lambda e: e.tensor_scalar(out=neglam[:], in0=neglam[:], scalar1=-float(LAMBDA_INIT), scalar2=None, op0=ALU.add), writes=["neglam"])
        for r_ in ["tri", "ident", "posf", "neghalf", "neglam", "Vaug"]:
            tr.barrier_all(r_)
            del tr.last_w[r_]
            tr.readers.pop(r_, None)

        for r_ in ["lams", "lamt"]:
            tr.barrier_all(r_)
        es_init.close()
        KTp = [sb(f"KTp{i}", [128, 512], BF16) for i in range(2)]
        Vp = [sb(f"Vp{i}", [128, 4, 130], BF16) for i in range(2)]
        ob = sb("ob", [128, 2, 130])
        dst_ = sb("dst", [128, 8])

        state = {"gu_mode": "D", "gu_half": 0, "lastS": (0, 1), "rot": 0, "tb": 0, "slab": 0, "gu": 0, "pjs": 0, "wbank": 0, "sbank": 0, "pt": 0, "kv": 0, "xbf": 0}

        CG_ORDER = [1, 2, 5, 6, 0, 3, 4]
        recording = plan_in is None
        plan = [] if recording else list(plan_in)
        ring_state = {"issued": 0, "consumed": 0, "done": 0}

        def gview(i):
            return wbuf[i][:].rearrange("p (dc c) -> p dc c", dc=8)

        def dview(i):
            return wbuf[i][:].rearrange("p (fc c) -> p fc c", fc=4)

        def _issue(k):
            kind, wn, c0, cw = plan[k]
            i = k % 4
            if kind == "g":
                tr.dma(out=gview(i)[:, :, :cw], in_=wb[wn][:, c0:c0 + cw].rearrange("(dc p) c -> p dc c", p=128),
                       reads=["w_" + wn], writes=[f"wb{i}"], sem=f"wb{i}")
            else:
                tr.dma(out=dview(i)[:, :cw // 128, :], in_=wb[wn][c0:c0 + cw, :].rearrange("(fc p) c -> p fc c", p=128),
                       reads=["w_" + wn], writes=[f"wb{i}"], sem=f"wb{i}")

        def need(kind, wn, c0, cw=512):
            k = ring_state["consumed"]
            if recording:
                plan.append((kind, wn, c0, cw))
            assert plan[k] == (kind, wn, c0, cw), (plan[k], kind, wn, c0, cw)
            ring_state["consumed"] += 1
            _pump()
            assert ring_state["issued"] > k, "weight-slab ring exhausted: more than 4 slabs live"
            live_k[k % 4] = k
            return k % 4

        def _pump():
            while ring_state["issued"] < min(len(plan), ring_state["done"] + 4):
                _issue(ring_state["issued"])
                ring_state["issued"] += 1

        live_k = {}
        done_set = set()

        def release(*bufs):
            for b_ in bufs:
                done_set.add(live_k[b_])
            while ring_state["done"] in done_set:
                done_set.discard(ring_state["done"])
                ring_state["done"] += 1
            _pump()

        def transpose_blocks(src_ap_fn, n, dst_ap, src_res, dst_res, evac="act"):
            b = state["tb"] % 2
            state["tb"] += 1
            pv = bkb(b)
            for j in range(n):
                tr.op("pe", lambda e, j=j: e.transpose(out=pv[:, j * 128:(j + 1) * 128], in_=src_ap_fn(j), identity=ident[:]),
                      reads=src_res, writes=[f"ps{b}"], signal=(j == n - 1))
            src = pv[:, 0:n * 128].rearrange("p (a b) -> p a b", a=n)
            if evac == "act":
                tr.op("act", lambda e: e.copy(out=dst_ap, in_=src), writes=[f"ps{b}"] + dst_res)
            else:
                tr.op("dve", lambda e: e.tensor_copy(out=dst_ap, in_=src), writes=[f"ps{b}"] + dst_res)

        def stream_to_T(slot, dstT, dpre, col0, scale_alpha, light_act=False):
            s = state["xbf"] % 2
            state["xbf"] += 1
            if light_act:
                tr.op("dve", lambda e: e.tensor_copy(out=xbf[s][:], in_=xf[:, slot, :]), reads=[f"xf{slot}"], writes=[f"xbf{s}"])
            else:
                tr.op("act", lambda e: e.copy(out=xbf[s][:], in_=xf[:, slot, :]), reads=[f"xf{slot}"], writes=[f"xbf{s}"])
            ev = "dve"
            transpose_blocks(lambda j: xbf[s][:, j * 128:(j + 1) * 128], 8, dstT[:, :, col0:col0 + 128],
                             [f"xbf{s}"], [f"{dpre}{col0 // 128}"], evac=ev)
            if scale_alpha:
                if light_act:
                    tr.op("dve", lambda e: e.tensor_scalar(out=xf[:, slot, :], in0=xf[:, slot, :], scalar1=float(ALPHA), scalar2=None,
                                                            op0=ALU.mult), writes=[f"xf{slot}"])
                else:
                    tr.op("act", lambda e: e.mul(out=xf[:, slot, :], in_=xf[:, slot, :], mul=float(ALPHA)), writes=[f"xf{slot}"])

        def layer_norm(blk, iw, ib):
            src = xf[:, blk, :]
            res = f"xf{blk}"
            tr.op("dve", lambda e: e.bn_stats(out=bst[:, 0, :], in_=xf[:, blk, 0:512]), reads=[res], writes=["bst"])
            tr.op("dve", lambda e: e.bn_stats(out=bst[:, 1, :], in_=xf[:, blk, 512:1024]), reads=[res], writes=["bst"])
            tr.op("dve", lambda e: e.bn_aggr(out=mv[:], in_=bst[:].rearrange("p a b -> p (a b)")), reads=["bst"], writes=["mv"])
            tr.op("dve", lambda e: e.tensor_scalar(out=rstd[:], in0=mv[:, 1:2], scalar1=float(LN_EPS), scalar2=None, op0=ALU.add),
                  reads=["mv"], writes=["rstd"])
            tr.op("pool", lambda e: e.tensor_tensor(out=rstd[:], in0=rstd[:], in1=neghalf[:, 0:1], op=ALU.pow), writes=["rstd"])
            tr.op("dve", lambda e: e.scalar_tensor_tensor(out=src, in0=src, scalar=mv[:, 0:1], in1=lnv[:, iw, :],
                                                          op0=ALU.subtract, op1=ALU.mult), reads=["mv"], writes=[res])
            tr.op("dve", lambda e: e.scalar_tensor_tensor(out=src, in0=src, scalar=rstd[:, 0:1], in1=lnv[:, ib, :],
                                                          op0=ALU.mult, op1=ALU.add), reads=["rstd"], writes=[res])

        def ffn_gu(wn_g, wn_u, ntok, srcT, src_res, inter=False):
            nslab = 6
            for j in range(nslab):
                cw = 512 if j < 5 else 256
                ig = need("g", wn_g, j * 512, cw)
                iu = need("g", wn_u, j * 512, cw)
                wgv, wuv = gview(ig), gview(iu)
                for k in range(cw // 128):
                    fc = j * 4 + k
                    q = state["gu"] % 2
                    state["gu"] += 1
                    mode = state["gu_mode"] if inter else "N"
                    if mode == "D":
                        bg, bu = (6, 7) if q == 0 else (0, 1)
                    elif mode == "B":
                        bg, bu = 6, 7
                    elif mode == "C":
                        bg, bu = 0, 1
                    else:
                        bg, bu = (2, 3) if q == 0 else (4, 5)
                    state["gu_half"] = 1
                    for dc in range(8):
                        tr.op("pe", lambda e, dc=dc: e.matmul(bk(bg)[:, :ntok], lhsT=wgv[:, dc, k * 128:(k + 1) * 128],
                                                              rhs=srcT[:, dc, :ntok], start=(dc == 0), stop=(dc == 7)),
                              reads=[f"wb{ig}"] + src_res, writes=[f"ps{bg}"], signal=(dc == 7))
                    if inter:
                        yield
                    for dc in range(8):
                        tr.op("pe", lambda e, dc=dc: e.matmul(bk(bu)[:, :ntok], lhsT=wuv[:, dc, k * 128:(k + 1) * 128],
                                                              rhs=srcT[:, dc, :ntok], start=(dc == 0), stop=(dc == 7)),
                              reads=[f"wb{iu}"] + src_res, writes=[f"ps{bu}"], signal=(dc == 7))
                    sq = sg[q][:, :ntok]
                    if mode == "D":
                        tr.op("act", lambda e: e.activation(out=sq, in_=bk(bg)[:, :ntok], func=AF.Tanh, scale=0.5),
                              reads=[f"ps{bg}"], writes=[f"sg{q}"])
                        tr.op("dve", lambda e: e.scalar_tensor_tensor(out=sq, in0=sq, scalar=1.0, in1=bk(bg)[:, :ntok],
                                                                      op0=ALU.add, op1=ALU.mult), writes=[f"ps{bg}", f"sg{q}"])
                        tr.op("dve", lambda e: e.scalar_tensor_tensor(out=hT[:, fc, :ntok], in0=sq, scalar=0.5, in1=bk(bu)[:, :ntok],
                                                                      op0=ALU.mult, op1=ALU.mult),
                              reads=[f"sg{q}"], writes=[f"ps{bu}", f"hT{fc}"])
                    else:
                        tr.op("act", lambda e: e.activation(out=sq, in_=bk(bg)[:, :ntok], func=AF.Silu),
                              writes=[f"ps{bg}", f"sg{q}"])
                        tr.op("dve", lambda e: e.tensor_tensor(out=hT[:, fc, :ntok], in0=sq, in1=bk(bu)[:, :ntok], op=ALU.mult),
                              reads=[f"sg{q}"], writes=[f"ps{bu}", f"hT{fc}"])
                        if fc % 2 == 1:
                            drain_casts(1, dep=f"hT{fc}")
                    state["gu_half"] = 0
                    if inter:
                        yield
                release(ig, iu)

        def ffn_down(wn_d, blks, post=None, between=None):
            nslab = 6
            npass = (len(blks) + 1) // 2
            for ps_ in range(npass):
                pblks = blks[ps_ * 2:ps_ * 2 + 2]
                bsets = [(2, 3), (4, 5)] if ps_ % 2 == 0 else [(0, 1), (6, 7)]
                for j in range(nslab):
                    cw = 512 if j < 5 else 256
                    s = need("d", wn_d, j * 512, cw)
                    wdv = dview(s)
                    for bi, (blk, tcol) in enumerate(pblks):
                        for k in range(cw // 128):
                            fc = j * 4 + k
                            for half in range(2):
                                b = bsets[bi][half]
                                tr.op("pe", lambda e, b=b, half=half, fc=fc, k=k, tcol=tcol, wdv=wdv: e.matmul(
                                    bk(b), lhsT=hT[:, fc, tcol:tcol + 128], rhs=wdv[:, k, half * 512:(half + 1) * 512],
                                    start=(fc == 0), stop=(fc == NFC - 1)),
                                    reads=[f"wb{s}", f"hT{fc}"], writes=[f"ps{b}"], signal=(fc == NFC - 1 or k == cw // 128 - 1))
                    release(s)
                    drain_casts(1)
                    if between is not None and ps_ == 0 and j == 3:
                        between()
                for bi, (blk, tcol) in enumerate(pblks):
                    for half in range(2):
                        b = bsets[bi][half]
                        dst = xf[:, blk, half * 512:(half + 1) * 512]
                        tr.op("dve", lambda e, b=b, dst=dst: e.scalar_tensor_tensor(out=dst, in0=bk(b), scalar=0.5, in1=dst,
                                                                                    op0=ALU.mult, op1=ALU.add),
                              writes=[f"ps{b}", f"xf{blk}"])
                    if post is not None and ps_ < npass - 1:
                        post(blk)

        def rot_tables(t, defer_sin=False):
            twopi = 2.0 * math.pi
            c1 = float(np.float32(6.28125))
            c2 = float(np.float32(twopi - c1))
            c3 = float(twopi - c1 - c2)
            V = "dve"
            PJ3 = ["pj0", "pj1", "pj2"]
            tr.op(V, lambda e: e.tensor_tensor(out=ang, in0=posf[:, 4 * t:4 * t + 4].unsqueeze(2).to_broadcast([128, 4, 64]),
                                               in1=inv[:].unsqueeze(1).to_broadcast([128, 4, 64]), op=ALU.mult), writes=PJ3)
            tr.op(V, lambda e: e.tensor_scalar(out=angi, in0=ang, scalar1=float(1.0 / twopi), scalar2=None, op0=ALU.mult),
                  writes=PJ3)
            tr.op(V, lambda e: e.tensor_copy(out=angk, in_=angi), writes=PJ3)
            tr.op(V, lambda e: e.scalar_tensor_tensor(out=rsn, in0=angk, scalar=-c1, in1=ang, op0=ALU.mult, op1=ALU.add),
                  writes=PJ3)
            tr.op(V, lambda e: e.scalar_tensor_tensor(out=rsn, in0=angk, scalar=-c2, in1=rsn, op0=ALU.mult, op1=ALU.add),
                  writes=PJ3)
            tr.op(V, lambda e: e.scalar_tensor_tensor(out=rsn, in0=angk, scalar=-c3, in1=rsn, op0=ALU.mult, op1=ALU.add),
                  writes=PJ3)

            def wrap(dst, dres, src, sres, shift):
                tr.op(V, lambda e: e.tensor_scalar(out=dst, in0=src, scalar1=float(shift), scalar2=None, op0=ALU.add),
                      writes=PJ3)
                tr.op(V, lambda e: e.tensor_scalar(out=angm, in0=dst, scalar1=float(-math.pi), scalar2=float(twopi),
                                                   op0=ALU.is_lt, op1=ALU.mult), writes=PJ3)
                tr.op(V, lambda e: e.tensor_tensor(out=dst, in0=dst, in1=angm, op=ALU.add), writes=PJ3)
                tr.op(V, lambda e: e.tensor_scalar(out=angm, in0=dst, scalar1=float(math.pi), scalar2=float(-twopi),
                                                   op0=ALU.is_gt, op1=ALU.mult), writes=PJ3)
                tr.op(V, lambda e: e.tensor_tensor(out=dst, in0=dst, in1=angm, op=ALU.add), writes=PJ3)
                tr.op(V, lambda e: e.tensor_scalar(out=dst, in0=dst, scalar1=float(math.pi), scalar2=float(-math.pi),
                                                   op0=ALU.min, op1=ALU.max), writes=PJ3)
            wrap(rsn, "rsn", rsn, "rsn", 0.0)
            wrap(rcs, "rcs", rsn, "rsn", math.pi / 2)
            if not defer_sin:
                rot_sin()

        def rot_sin():
            tr.op("act", lambda e: e.activation(out=sn[:], in_=rsn, func=AF.Sin), writes=["pj2", "sn"])
            tr.op("act", lambda e: e.activation(out=cs[:], in_=rcs, func=AF.Sin), writes=["pj2", "cs"])

        def rotate(src, sres, dst, dres, blk, kind):
            if kind == "ret":
                sv = src.rearrange("p (h i two) -> p h i two", h=8, i=32, two=2)
                dv = dst.rearrange("p (h i two) -> p h i two", h=8, i=32, two=2)
                a, b_ = sv[:, :, :, 0], sv[:, :, :, 1]
                oa, ob_ = dv[:, :, :, 0], dv[:, :, :, 1]
                c = cs[:, blk, 0:32].unsqueeze(1).to_broadcast([128, 8, 32])
                s_ = sn[:, blk, 0:32].unsqueeze(1).to_broadcast([128, 8, 32])
            else:
                sv = src.rearrange("p (h two i) -> p h two i", h=8, two=2, i=32)
                dv = dst.rearrange("p (h two i) -> p h two i", h=8, two=2, i=32)
                a, b_ = sv[:, :, 0, :], sv[:, :, 1, :]
                oa, ob_ = dv[:, :, 0, :], dv[:, :, 1, :]
                c = cs[:, blk, 32:64].unsqueeze(1).to_broadcast([128, 8, 32])
                s_ = sn[:, blk, 32:64].unsqueeze(1).to_broadcast([128, 8, 32])
            t = [r_[:].rearrange("p (h i) -> p h i", h=8) for r_ in rt]
            tr.op("dve", lambda e: e.tensor_tensor(out=t[0], in0=a, in1=c, op=ALU.mult), reads=[sres, "cs"], writes=["rt0"])
            tr.op("dve", lambda e: e.tensor_tensor(out=t[1], in0=b_, in1=s_, op=ALU.mult), reads=[sres, "sn"], writes=["rt1"])
            tr.op("dve", lambda e: e.tensor_tensor(out=t[2], in0=b_, in1=c, op=ALU.mult), reads=[sres, "cs"], writes=["rt2"])
            tr.op("dve", lambda e: e.tensor_tensor(out=t[3], in0=a, in1=s_, op=ALU.mult), reads=[sres, "sn"], writes=["rt3"])
            tr.op("dve", lambda e: e.tensor_tensor(out=oa, in0=t[0], in1=t[1], op=ALU.subtract), reads=["rt0", "rt1"], writes=[dres])
            tr.op("dve", lambda e: e.tensor_tensor(out=ob_, in0=t[2], in1=t[3], op=ALU.add), reads=["rt2", "rt3"], writes=[dres])

        def win_slab(cg):
            return need("g", "win", cg * 512)

        def proj(s, tcol):
            b = 2 + state["wbank"] % 4
            state["wbank"] += 1
            for dc in range(8):
                tr.op("pe", lambda e, dc=dc: e.matmul(bk(b), lhsT=xTB[:, dc, tcol:tcol + 128], rhs=gview(s)[:, dc, :],
                                                      start=(dc == 0), stop=(dc == 7)),
                      reads=[f"wb{s}", f"xTB{tcol // 128}"], writes=[f"ps{b}"], signal=(dc == 7))
            return b

        def evac_f32(b):
            i = state["pjs"] % 3
            state["pjs"] += 1
            tr.op("act", lambda e: e.copy(out=pj[i][:], in_=bk(b)), writes=[f"ps{b}", f"pj{i}"])
            return i

        OWN_SLOTS = [(0, 1), (2, 3)]
        own_blks = [0, 2]

        def slots(t):
            o = OWN_SLOTS[t % 2]
            return [o[0], 4, o[1], 5]

        def load_x(t):
            sl4 = slots(t)
            for blk in range(4):
                L = 4 * t + blk
                tr.dma(out=xf[:, sl4[blk], :], in_=x_d[L * 128:(L + 1) * 128, :], writes=[f"xf{sl4[blk]}"], sem=f"xf{sl4[blk]}")

        def ffn(wn_g, wn_u, wn_d, ntok, blks, srcT, src_res, post=None, between=None):
            for _ in ffn_gu(wn_g, wn_u, ntok, srcT, src_res):
                pass
            ffn_down(wn_d, blks, post=post, between=between)

        def prep_A_blk(t, blk, light_act=False):
            sl4 = slots(t)
            stream_to_T(sl4[blk], xTA, "xTA", blk * 128, True, light_act=light_act)

        def prep_A(t, light_act=False):
            rot_tables(t)
            for blk in range(4):
                prep_A_blk(t, blk, light_act=light_act)

        def ffn_A_gu(t, inter=False):
            return ffn_gu("wg1", "wu1", 512, xTA, [f"xTA{i}" for i in range(4)], inter=inter)

        def ffn_A_down(t, between=None):
            sl4 = slots(t)
            ffn_down("wd1", [(sl4[b_], b_ * 128) for b_ in range(4)], post=lambda slot: layer_norm(slot, 0, 1), between=between)
            layer_norm(sl4[2], 0, 1)
            layer_norm(sl4[3], 0, 1)

        def x1_to_T(t):
            sl4 = slots(t)
            for blk in range(4):
                stream_to_T(sl4[blk], xTB, "xTB", blk * 128, blk in own_blks)

        def x2_to_T(t):
            sl4 = slots(t)
            for qi, blk in enumerate(own_blks):
                stream_to_T(sl4[blk], xTB, "xTB", qi * 128, True)

        def stage_BE(t, before_C=None, before_D=None, inter_gen=None):
            sl4 = slots(t)
            if t + 1 < nt:
                load_x(t + 1)
            items = []
            for blk in range(4):
                items.append(("rk", 1, blk, None))
            for blk in range(4):
                items.append(("rv", 2, blk, None))
            for blk in range(4):
                items.append(("dk", 5, blk, None))
            for blk in range(4):
                items.append(("dv", 6, blk, None))
            for qi, blk in enumerate(own_blks):
                items.append(("rq", 0, blk, qi))
            for qi, blk in enumerate(own_blks):
                items.append(("rg", 3, blk, qi))
            for qi, blk in enumerate(own_blks):
                items.append(("dq", 4, blk, qi))
            cg_order = [1, 2, 5, 6, 0, 3, 4]
            slab_of = {}

            prev_cg = [None]

            def issue_slab(cg):
                slab_of[cg] = win_slab(cg)
                prev_cg[0] = cg

            st_ = {}

            def P(it):
                kind, cg, blk, qi = it
                if cg not in slab_of:
                    if slab_of:
                        release(slab_of[prev_cg[0]])
                    issue_slab(cg)
                nxt = cg_order.index(cg) + 1
                if nxt < len(cg_order) and cg_order[nxt] not in slab_of and blk == (3 if qi is None else own_blks[-1]):
                    pass
                st_[it] = dict(bank=proj(slab_of[cg], blk * 128))

            def V(it):
                kind, cg, blk, qi = it
                b = st_[it]["bank"]
                if kind == "rv":
                    tr.op("act", lambda e: e.copy(out=rvb[:, blk, :], in_=bk(b)), writes=[f"ps{b}", f"rvb{blk}"])
                elif kind == "dv":
                    tr.op("act", lambda e: e.copy(out=Vaug[:, blk, :, 0:128], in_=bk(b).rearrange("p (h c) -> p h c", h=4)),
                          writes=[f"ps{b}", f"Vaug{blk}"])
                elif kind == "rg":
                    tr.op("act", lambda e: e.activation(out=sgr[:, qi, :], in_=bk(b), func=AF.Silu), writes=[f"ps{b}", f"sgr{qi}"])
                else:
                    i = evac_f32(b)
                    r = state["rot"] % 2
                    state["rot"] += 1
                    st_[it]["rot"] = r
                    rotate(pj[i][:], f"pj{i}", rot[r][:], f"rot{r}", blk, "ret" if kind in ("rk", "rq") else "rope")
                    if kind == "rk":
                        which = 0 if blk in own_blks else 1
                        tr.op("dve", lambda e: e.tensor_tensor(
                            out=Kdec[:, blk, :].rearrange("p (h d) -> p h d", h=8), in0=rot[r][:].rearrange("p (h d) -> p h d", h=8),
                            in1=kdec[:, which, :].unsqueeze(2).to_broadcast([128, 8, 64]), op=ALU.mult),
                            reads=[f"rot{r}"], writes=[f"Kdec{blk}"])

            def T(it):
                kind, cg, blk, qi = it
                if kind in ("rv", "dv", "rg"):
                    return
                r = st_[it]["rot"]
                src = lambda j: rot[r][:, j * 128:(j + 1) * 128]
                if kind == "rk":
                    transpose_blocks(src, 4, rkT[:, :, blk * 128:(blk + 1) * 128], [f"rot{r}"], [f"rkT{blk}"], evac="act")
                elif kind == "dk":
                    transpose_blocks(src, 4, dkT[:, :, blk * 128:(blk + 1) * 128], [f"rot{r}"], [f"dkT{blk}"], evac="act")
                elif kind == "rq":
                    transpose_blocks(src, 4, rqT[:, :, qi * 128:(qi + 1) * 128], [f"rot{r}"], [f"rqT{qi}"], evac="act")
                    tr.op("dve", lambda e: e.tensor_tensor(out=rqTd[:, :, qi * 128:(qi + 1) * 128], in0=rqT[:, :, qi * 128:(qi + 1) * 128],
                                                            in1=qdec[:], op=ALU.mult), reads=[f"rqT{qi}"], writes=[f"rqTd{qi}"])
                elif kind == "dq":
                    transpose_blocks(src, 4, dqT[:, :, qi * 128:(qi + 1) * 128], [f"rot{r}"], [f"dqT{qi}"], evac="act")

            def pump(n):
                if inter_gen is not None:
                    for _ in range(n):
                        next(inter_gen, None)

            state["gu_mode"] = "B"
            n_it = len(items)
            for idx in range(n_it + 2):
                if idx < n_it:
                    cg = items[idx][1]
                    P(items[idx])
                    if inter_gen is not None:
                        if 4 <= idx < 8:
                            prep_A_blk(t + 1, idx - 4)
                if 0 <= idx - 1 < n_it:
                    V(items[idx - 1])
                if 0 <= idx - 2 < n_it:
                    T(items[idx - 2])
            release(slab_of[prev_cg[0]])
            if before_C is not None:
                before_C()
            for h in range(4):
                tr.dma(out=KTd[h, :, t * 512:(t + 1) * 512], in_=dkT[:, h, :], reads=[f"dkT{i}" for i in range(4)],
                       writes=[f"kvd{t}"], sem=f"kvw{t % 2}")
                tr.dma(out=Vd[h, :, 4 * t:4 * t + 4, :], in_=Vaug[:, :, h, :], reads=[f"Vaug{i}" for i in range(4)],
                       writes=[f"kvd{t}"], sem=f"kvw{t % 2}")
            if state["gu_half"] == 1:
                pump(1)
            state["gu_mode"] = "C"
            if before_D is not None:
                rot_tables(t + 1, defer_sin=True)
            for qi in range(2):
                ob_, xb_ = 2 * qi, 2 * qi + 1
                for (kb, bA, bB, Dt, At, ares) in [(ob_, 2, 3, DoT, AoT, "AoT"), (xb_, 4, 5, DxT, AxT, "AxT")]:
                    for h in range(8):
                        i_, hh = h // 2, h % 2
                        b = bA if hh == 0 else bB
                        tr.op("pe", lambda e, b=b, h=h, i_=i_, hh=hh, kb=kb: e.matmul(
                            bk(b)[:, i_ * 128:(i_ + 1) * 128],
                            lhsT=rkT[hh * 64:(hh + 1) * 64, i_, kb * 128:(kb + 1) * 128],
                            rhs=rqT[hh * 64:(hh + 1) * 64, i_, qi * 128:(qi + 1) * 128], start=True, stop=True),
                            reads=[f"rkT{kb}", f"rqT{qi}"], writes=[f"ps{b}"], signal=(h >= 6))
                    for half, b in enumerate([bA, bB]):
                        tr.op("dve", lambda e, b=b, half=half, Dt=Dt, At=At: e.tensor_tensor(
                            out=At[:, half * 4:(half + 1) * 4, :], in0=bk(b).rearrange("p (h q) -> p h q", h=4),
                            in1=Dt[:, half * 4:(half + 1) * 4, :], op=ALU.mult), writes=[f"ps{b}", ares + str(half)])
                first = True
                for h in range(8):
                    i_, hh = h // 2, h % 2
                    half = hh
                    sl_ = hh * 4 + i_
                    oc = bk(6)[:, h * 64:(h + 1) * 64]
                    tr.op("pe", lambda e, oc=oc, h=h, first=first, sl_=sl_: e.matmul(oc, lhsT=AoT[:, sl_, :], rhs=rvb[:, ob_, h * 64:(h + 1) * 64],
                                                                            start=first, stop=False, skip_group_check=True),
                          reads=[f"AoT{half}", f"rvb{ob_}"], writes=["ps6"], signal=False)
                    first = False
                    tr.op("pe", lambda e, oc=oc, h=h, sl_=sl_: e.matmul(oc, lhsT=AxT[:, sl_, :], rhs=rvb[:, xb_, h * 64:(h + 1) * 64],
                                                               start=False, stop=False, skip_group_check=True),
                          reads=[f"AxT{half}", f"rvb{xb_}"], writes=["ps6"], signal=False)
                    tr.op("pe", lambda e, oc=oc, h=h, i_=i_, hh=hh: e.matmul(
                        oc, lhsT=rqTd[hh * 64:(hh + 1) * 64, i_, qi * 128:(qi + 1) * 128], rhs=Sbf[hh * 64:(hh + 1) * 64, i_, :],
                        start=False, stop=(h == 7), skip_group_check=True),
                        reads=[f"rqTd{qi}", "Sbf"], writes=["ps6"], signal=(h == 7))
                pump(3)
                for i_ in range(4):
                    for n_, kb in enumerate([ob_, xb_]):
                        tr.op("pe", lambda e, i_=i_, kb=kb, n_=n_: e.matmul(
                            bk(7)[:, i_ * 128:(i_ + 1) * 128], lhsT=Kdec[:, kb, i_ * 128:(i_ + 1) * 128],
                            rhs=rvb[:, kb, i_ * 128:(i_ + 1) * 128], start=(i_ == 0 and n_ == 0), stop=(i_ == 3 and n_ == 1),
                            skip_group_check=True),
                            reads=[f"Kdec{kb}", f"rvb{kb}"], writes=["ps7"], signal=(i_ == 3 and n_ == 1))
                tr.op("dve", lambda e: e.tensor_tensor(out=Sst[:], in0=Sst[:], in1=Gt[:], op=ALU.mult), reads=["Sbf"], writes=["Sst"])
                for hh in range(2):
                    tr.op("dve", lambda e, hh=hh: e.tensor_tensor(
                        out=Sst[hh * 64:(hh + 1) * 64, :, :], in0=Sst[hh * 64:(hh + 1) * 64, :, :],
                        in1=bk(7)[hh * 64:(hh + 1) * 64, :].rearrange("p (i c) -> p i c", i=4)[:, :, hh * 64:(hh + 1) * 64],
                        op=ALU.add), writes=["ps7", "Sst"])
                tr.op("dve", lambda e: e.tensor_copy(out=Sbf[:], in_=Sst[:]), reads=["Sst"], writes=["Sbf"])
                pump(3)
                tr.op("act", lambda e: e.copy(out=yr, in_=bk(6).rearrange("p (h c) -> p h c", h=8)), writes=["ps6", "pj1"])
                tr.op("dve", lambda e: e.tensor_reduce(out=hst[:, 0, :], in_=yr, axis=AX.X, op=ALU.add), reads=["pj1"], writes=["hst0"])
                tr.op("dve", lambda e: e.tensor_tensor(out=ysq, in0=yr, in1=yr, op=ALU.mult), reads=["pj1"], writes=["pj0"])
                tr.op("dve", lambda e: e.tensor_reduce(out=hst[:, 1, :], in_=ysq, axis=AX.X, op=ALU.add), reads=["pj0"], writes=["hst1"])
                tr.op("dve", lambda e: e.tensor_scalar(out=hst[:, 2, :], in0=hst[:, 0, :], scalar1=1.0 / 64.0, scalar2=None, op0=ALU.mult),
                      reads=["hst0"], writes=["hst2"])
                tr.op("dve", lambda e: e.tensor_tensor(out=hst[:, 0, :], in0=hst[:, 2, :], in1=hst[:, 2, :], op=ALU.mult),
                      reads=["hst2"], writes=["hst0"])
                tr.op("dve", lambda e: e.scalar_tensor_tensor(out=hst[:, 3, :], in0=hst[:, 1, :], scalar=1.0 / 64.0, in1=hst[:, 0, :],
                                                              op0=ALU.mult, op1=ALU.subtract), reads=["hst0", "hst1"], writes=["hst3"])
                tr.op("dve", lambda e: e.tensor_scalar(out=hst[:, 3, :], in0=hst[:, 3, :], scalar1=float(NORM_EPS), scalar2=None, op0=ALU.add),
                      writes=["hst3"])
                tr.op("pool", lambda e: e.tensor_tensor(out=hst[:, 3, :], in0=hst[:, 3, :], in1=neghalf[:], op=ALU.pow), writes=["hst3"])
                tr.op("dve", lambda e: e.tensor_tensor(out=yr, in0=yr, in1=hst[:, 2, :].unsqueeze(2).to_broadcast([128, 8, 64]),
                                                       op=ALU.subtract), reads=["hst2"], writes=["pj1"])
                tr.op("dve", lambda e: e.tensor_tensor(out=yr, in0=yr, in1=hst[:, 3, :].unsqueeze(2).to_broadcast([128, 8, 64]),
                                                       op=ALU.mult), reads=["hst3"], writes=["pj1"])
                tr.op("dve", lambda e: e.tensor_tensor(out=yr, in0=yr, in1=nw[:, 0, :].rearrange("p (h c) -> p h c", h=8),
                                                        op=ALU.mult), writes=["pj1"])
                tr.op("dve", lambda e, qi=qi: e.tensor_tensor(out=merged[:, qi, 0:512], in0=yr.rearrange("p h c -> p (h c)"),
                                                               in1=sgr[:, qi, :], op=ALU.mult), reads=["pj1", f"sgr{qi}"], writes=[f"mgr{qi}"])

            pump(2)
            if state["gu_half"] == 1:
                pump(1)
            state["gu_mode"] = "D"
            if before_D is not None:
                rot_sin()
            obank = {(0, 0): (4, 0), (0, 1): (4, 256), (1, 0): (5, 0), (1, 1): (5, 256)}
            gen_live = [inter_gen is not None]
            for h in range(4):
                started = set()
                steps = []
                for c in range(t):
                    steps.append(("hist", c, (0, 1)))
                    steps.append(("hist", c, (2, 3)))
                steps.append(("cur", t, (0, 1)))
                steps.append(("cur", t, (2, 3)))
                cur_k = (dkT[:, h, :], [f"dkT{i}" for i in range(4)])
                cur_v = (Vaug[:, :, h, :], [f"Vaug{i}" for i in range(4)])
                slot_of = {}
                info = {}

                def do_qk(si):
                    kind, c, kbs = steps[si]
                    if kind == "hist":
                        if c not in slot_of:
                            sl = state["kv"] % 2
                            state["kv"] += 1
                            slot_of[c] = sl
                            tr.dma(out=KTp[sl][:], in_=KTd[h, :, c * 512:(c + 1) * 512], reads=[f"kvd{c}"], writes=[f"KTp{sl}"], sem=f"KTp{sl}")
                            tr.dma(out=Vp[sl][:], in_=Vd[h, :, 4 * c:4 * c + 4, :], reads=[f"kvd{c}"], writes=[f"Vp{sl}"], sem=f"Vp{sl}")
                        sl = slot_of[c]
                        kt_ap, kres = KTp[sl], [f"KTp{sl}"]
                        vt = (Vp[sl], [f"Vp{sl}"])
                    else:
                        kt_ap, kres = cur_k
                        vt = cur_v
                    if gen_live[0]:
                        b = (2, 3)
                    else:
                        b = (0, 1) if state["lastS"] == (2, 3) else (2, 3)
                    state["lastS"] = b
                    only_b = (kind == "cur" and kbs == (2, 3))
                    for j, kb in enumerate(kbs):
                        for sub in range(2):
                            if only_b:
                                o_ = bk(b[sub])[:, j * 256 + 128:(j + 1) * 256]
                                r_ = dqT[sub * 64:(sub + 1) * 64, h, 128:256]
                                rres = ["dqT1"]
                            else:
                                o_ = bk(b[sub])[:, j * 256:(j + 1) * 256]
                                r_ = dqT[sub * 64:(sub + 1) * 64, h, :]
                                rres = ["dqT0", "dqT1"]
                            tr.op("pe", lambda e, o_=o_, r_=r_, sub=sub, kb=kb, kt_ap=kt_ap: e.matmul(
                                o_, lhsT=kt_ap[sub * 64:(sub + 1) * 64, kb * 128:(kb + 1) * 128], rhs=r_, start=True, stop=True),
                                reads=kres + rres, writes=[f"ps{b[sub]}"], signal=(j == 1))
                    info[si] = (b, vt)

                def ex(b, p, sub, c0, c1, special):
                    kw = dict(bias=sbias[:, 0:1]) if special else {}
                    tr.op("act", lambda e: e.activation(out=PT[p][:, sub * 512 + c0:sub * 512 + c1], in_=bk(b[sub])[:, c0:c1],
                                                        func=AF.Exp, scale=0.125, **kw),
                          writes=[f"ps{b[sub]}", f"PT{p}"])

                def do_exp(si):
                    kind, c, kbs = steps[si]
                    b, (v_ap, vres) = info[si]
                    p = state["pt"] % 2
                    state["pt"] += 1
                    ptv = PT[p][:].rearrange("p (s q) -> p s q", s=2)
                    if kind == "hist":
                        for sub in range(2):
                            ex(b, p, sub, 0, 512, False)
                        plan = [(0, kbs[0], [0, 1], []), (1, kbs[1], [0, 1], [])]
                    elif kbs == (0, 1):
                        for sub in range(2):
                            ex(b, p, sub, 0, 256, False)
                            ex(b, p, sub, 256, 384, True)
                            ex(b, p, sub, 384, 512, False)
                        tr.op("dve", lambda e: e.tensor_tensor(out=ptv[:, :, 0:128], in0=ptv[:, :, 0:128],
                                                               in1=tri[:].unsqueeze(1).to_broadcast([128, 2, 128]), op=ALU.mult),
                              writes=[f"PT{p}"])
                        plan = [(0, 0, [0, 1], []), (1, 1, [0, 1], [0])]
                    else:
                        for sub in range(2):
                            ex(b, p, sub, 128, 256, False)
                            ex(b, p, sub, 384, 512, True)
                        tr.op("dve", lambda e: e.tensor_tensor(out=ptv[:, :, 128:256], in0=ptv[:, :, 128:256],
                                                               in1=tri[:].unsqueeze(1).to_broadcast([128, 2, 128]), op=ALU.mult),
                              writes=[f"PT{p}"])
                        plan = [(0, 2, [1], []), (1, 3, [1], [1])]
                    info[si] = (b, (v_ap, vres), p, plan)

                def do_pv(si):
                    b, (v_ap, vres), p, plan = info[si]
                    for (j, kb, qis, last_for) in plan:
                        for sub in range(2):
                            for qi in qis:
                                ob__, oc0 = obank[(sub, qi)]
                                st = ob__ not in started
                                started.add(ob__)
                                c0 = sub * 512 + j * 256 + qi * 128
                                tr.op("pe", lambda e, ob__=ob__, oc0=oc0, c0=c0, st=st, kb=kb: e.matmul(
                                    bk(ob__)[:, oc0:oc0 + 129], lhsT=PT[p][:, c0:c0 + 128], rhs=v_ap[:, kb, 0:129],
                                    start=st, stop=False, skip_group_check=True),
                                    reads=[f"PT{p}"] + vres, writes=[f"ps{ob__}"], signal=(sub == 1 and qi == qis[-1]))

                qk_done = set()
                for si in range(len(steps)):
                    if si not in qk_done:
                        do_qk(si)
                        qk_done.add(si)
                    if gen_live[0]:
                        do_exp(si)
                        if next(inter_gen, "done") == "done":
                            gen_live[0] = False
                        do_pv(si)
                    else:
                        if si + 1 < len(steps):
                            do_qk(si + 1)
                            qk_done.add(si + 1)
                        do_exp(si)
                        do_pv(si)
                for qi in range(2):
                    for sub in range(2):
                        ob__, oc0 = obank[(sub, qi)]
                        tr.op("act", lambda e, ob__=ob__, oc0=oc0, sub=sub: e.copy(out=ob[:, sub, 0:129], in_=bk(ob__)[:, oc0:oc0 + 129]),
                              writes=[f"ps{ob__}", "ob"])
                    tr.op("dve", lambda e: e.reciprocal(out=dst_[:, 0:2], in_=ob[:, :, 128]), reads=["ob"], writes=["dst"])
                    tr.op("dve", lambda e: e.tensor_tensor(out=dst_[:, 2:3], in0=dst_[:, 1:2], in1=neglam[:], op=ALU.mult), writes=["dst"])
                    tr.op("dve", lambda e: e.tensor_scalar(out=dfa, in0=ob[:, 0, 0:128], scalar1=dst_[:, 0:1], scalar2=None, op0=ALU.mult),
                          reads=["ob", "dst"], writes=["pj2"])
                    tr.op("dve", lambda e: e.scalar_tensor_tensor(out=dfa, in0=ob[:, 1, 0:128], scalar=dst_[:, 2:3], in1=dfa,
                                                                  op0=ALU.mult, op1=ALU.add), reads=["ob", "dst"], writes=["pj2"])
                    tr.op("dve", lambda e: e.tensor_tensor(out=dfb, in0=dfa, in1=dfa, op=ALU.mult), reads=["pj2"], writes=["pj2"])
                    tr.op("dve", lambda e: e.tensor_reduce(out=dst_[:, 3:4], in_=dfb, axis=AX.X, op=ALU.add), reads=["pj2"], writes=["dst"])
                    tr.op("dve", lambda e: e.tensor_scalar(out=dst_[:, 3:4], in0=dst_[:, 3:4], scalar1=1.0 / 128.0, scalar2=float(NORM_EPS),
                                                           op0=ALU.mult, op1=ALU.add), writes=["dst"])
                    tr.op("pool", lambda e: e.tensor_tensor(out=dst_[:, 3:4], in0=dst_[:, 3:4], in1=neghalf[:, 0:1], op=ALU.pow), writes=["dst"])
                    tr.op("dve", lambda e: e.tensor_scalar(out=dfa, in0=dfa, scalar1=dst_[:, 3:4], scalar2=float(1.0 - LAMBDA_INIT),
                                                           op0=ALU.mult, op1=ALU.mult), reads=["dst"], writes=["pj2"])
                    tr.op("dve", lambda e, qi=qi, h=h: e.tensor_tensor(out=merged[:, qi, 512 + h * 128:512 + (h + 1) * 128], in0=dfa,
                                                                        in1=nw[:, 1, h * 128:(h + 1) * 128], op=ALU.mult),
                          reads=["pj2"], writes=[f"mgd{qi}_{h}"])

            if inter_gen is not None:
                for _ in inter_gen:
                    pass
            for qi in range(2):
                transpose_blocks(lambda j, qi=qi: merged[:, qi, j * 128:(j + 1) * 128], 8, xTB[:, :, qi * 128:(qi + 1) * 128],
                                 [f"mgr{qi}"] + [f"mgd{qi}_{h}" for h in range(4)], [f"xTB{qi}"], evac="act")
            for half in range(2):
                s = need("g", "wout", half * 512)
                for qi, blk in enumerate(own_blks):
                    b = 4 + half * 2 + qi
                    for mc in range(8):
                        tr.op("pe", lambda e, b=b, mc=mc, qi=qi: e.matmul(bk(b), lhsT=xTB[:, mc, qi * 128:(qi + 1) * 128], rhs=gview(s)[:, mc, :],
                                                                          start=(mc == 0), stop=(mc == 7)),
                              reads=[f"wb{s}", f"xTB{qi}"], writes=[f"ps{b}"], signal=(mc == 7))
                    dst = xf[:, sl4[blk], half * 512:(half + 1) * 512]
                    tr.op("dve", lambda e, b=b, dst=dst: e.tensor_tensor(out=dst, in0=bk(b), in1=dst, op=ALU.add),
                          writes=[f"ps{b}", f"xf{sl4[blk]}"])
                release(s)
            for blk in own_blks:
                layer_norm(sl4[blk], 2, 3)

        def stage_F(t, between=None):
            sl4 = slots(t)
            ffn("wg2", "wu2", "wd2", 256, [(sl4[blk], qi * 128) for qi, blk in enumerate(own_blks)], xTB, ["xTB0", "xTB1"],
                between=between)
            for qi, blk in enumerate(own_blks):
                layer_norm(sl4[blk], 4, 5)
                g = 2 * t + qi
                tr.dma(out=out_d[g * 128:(g + 1) * 128, :], in_=xf[:, sl4[blk], :], reads=[f"xf{sl4[blk]}"], writes=[],
                       sem=f"out{sl4[blk]}")

        load_x(0)
        prep_A(0)
        for _ in ffn_A_gu(0):
            pass
        ffn_A_down(0)
        drain_casts(len(pending_casts))
        x1_to_T(0)
        for t in range(nt):
            last = (t + 1 == nt)
            if last:
                stage_BE(t)
                x2_to_T(t)
                stage_F(t)
            else:
                stage_BE(t, before_D=lambda t=t: rot_tables(t + 1), inter_gen=ffn_A_gu(t + 1, inter=True))
                ffn_A_down(t + 1, between=lambda t=t: x2_to_T(t))
                stage_F(t, between=lambda t=t: x1_to_T(t + 1))

        for key in [k for k in tr.semh if k.startswith("dma:out")]:
            nc.sync.wait_ge(tr.semh[key], tr.cnt[key])
        if not recording:
            assert ring_state["consumed"] == len(plan)
            print("instructions:", tr.n_inst)
    if recording:
        return build_program(nt=nt, stop=stop, plan_in=plan)
    return nc


_NC_CACHE = {}


def _tables(p):
    h = np.arange(8, dtype=np.float64)
    gam = 1.0 - 2.0 ** (-5.0 - h)
    k = np.arange(128)[:, None]
    q = np.arange(128)[None, :]
    rel = (q - k).astype(np.float64)
    DoT = np.zeros((128, 8, 128), np.float64)
    DxT = np.zeros((128, 8, 128), np.float64)
    for hd in range(8):
        sl = (hd % 2) * 4 + hd // 2
        DoT[:, sl, :] = np.where(rel >= 0, 0.125 * gam[hd] ** np.maximum(rel, 0.0), 0.0)
        if p == 1:
            DxT[:, sl, :] = 0.125 * gam[hd] ** (128.0 + rel)
    qdec = np.zeros((128, 4, 128), np.float64)
    Gt = np.zeros((128, 4, 64), np.float64)
    for i in range(4):
        for hh in range(2):
            g_ = gam[2 * i + hh]
            qdec[hh * 64:(hh + 1) * 64, i, :] = (g_ ** (np.arange(128) + 1.0 + 128.0 * p))[None, :]
            Gt[hh * 64:(hh + 1) * 64, i, :] = g_ ** 256.0
    kdec = np.zeros((128, 2, 8), np.float64)
    tt = np.arange(128, dtype=np.float64)
    for hh in range(8):
        kdec[:, 0, hh] = 0.125 * gam[hh] ** (255.0 - (tt + 128.0 * p))
        kdec[:, 1, hh] = 0.125 * gam[hh] ** (255.0 - (tt + 128.0 * (1 - p)))
    tri = (np.arange(128)[:, None] <= np.arange(128)[None, :]).astype(np.float32)
    inv_ret = 1.0 / (10000.0 ** np.linspace(0.0, 1.0, 32, dtype=np.float32))
    inv_rope = 1.0 / (10000.0 ** (np.arange(0, 64, 2, dtype=np.float32) / 64.0))
    inv = np.concatenate([inv_ret.astype(np.float32), inv_rope.astype(np.float32)])[None, :].repeat(128, 0)
    sb = np.full((128, 1), 0.0 if p == 1 else NEG_BIG, np.float32)
    f = lambda a: np.ascontiguousarray(a.astype(np.float32))
    return dict(DoT=f(DoT), DxT=f(DxT), qdec=f(qdec), Gt=f(Gt), kdec=f(kdec), tri=f(tri),
                ident=np.eye(128, dtype=np.float32), inv=f(inv), sbias=sb)


def kernel(x, positions, ffn1_w_gate, ffn1_w_up, ffn1_w_down, ln1_w, ln1_b,
           w_in, ret_norm_w, diff_lambda_q1, diff_lambda_k1, diff_lambda_q2, diff_lambda_k2,
           diff_norm_w, w_out, ln2_w, ln2_b, ffn2_w_gate, ffn2_w_up, ffn2_w_down, ln3_w, ln3_b):
    if "nc" not in _NC_CACHE:
        _NC_CACHE["nc"] = build_program()
    nc = _NC_CACHE["nc"]
    A = lambda a: np.ascontiguousarray(np.asarray(a))
    x = A(x).astype(np.float32, copy=False)
    positions = A(positions).astype(np.int32, copy=False)
    rep = lambda v: np.ascontiguousarray(np.broadcast_to(np.asarray(v, np.float32).reshape(1, -1), (128, np.asarray(v).size)))
    lnv = np.ascontiguousarray(np.stack([rep(ln1_w), rep(ln1_b), rep(ln2_w), rep(ln2_b), rep(ln3_w), rep(ln3_b)], axis=1))
    nwv = np.ascontiguousarray(np.stack([rep(ret_norm_w), rep(diff_norm_w)], axis=1))
    lamv = np.ascontiguousarray(np.stack([rep(diff_lambda_q1), rep(diff_lambda_k1), rep(diff_lambda_q2), rep(diff_lambda_k2)], axis=1))
    shared = {
        "wg1": A(ffn1_w_gate)[0], "wu1": A(ffn1_w_up)[0], "wd1": A(ffn1_w_down)[0],
        "win": A(w_in)[0], "wout": A(w_out)[0],
        "wg2": A(ffn2_w_gate)[0], "wu2": A(ffn2_w_up)[0], "wd2": A(ffn2_w_down)[0],
        "lnv": lnv, "nw": nwv, "lam": lamv,
    }
    shared = {k: np.ascontiguousarray(v, dtype=np.float32) for k, v in shared.items()}
    tabs = [_tables(0), _tables(1)]
    in_maps = []
    for c in range(8):
        b, p = c // 2, c % 2
        order = []
        for g in range(NBLK // 2):
            order += [2 * g + p, 2 * g + 1 - p]
        xb = x[b].reshape(NBLK, 128, D)[order].reshape(SEQ, D)
        pb = positions[b].reshape(NBLK, 128)[order]
        m = dict(shared)
        m["x"] = np.ascontiguousarray(xb)
        m["pos"] = np.ascontiguousarray(pb.T)
        m.update(tabs[p])
        in_maps.append(m)
    res = run_bass_kernel_spmd(nc, in_maps, core_ids=list(range(8)))
    out = np.empty((BATCH, SEQ, D), np.float32)
    for c in range(8):
        b, p = c // 2, c % 2
        o = np.asarray(res.results[c]["out"]).reshape(NBLK // 2, 128, D)
        ov = out[b].reshape(NBLK, 128, D)
        for g in range(NBLK // 2):
            ov[2 * g + p] = o[g]
    return out
```
